# Optimizing a Trainium2 kernel written in Bass

```python
import jax, jax.numpy as jnp
from jax import lax
import numpy as np

D_MODEL = 2048
BATCH = 4
SEQ = 4096
DEPTH = 2

GRID_W = 64
CTX_LEN = 256
HEAD_DIM = 128
EPS = 1e-6
N_MOD = 9
D_FF = 5504
LRU_WIDTH = D_MODEL // 2
LRU_BLOCKS = LRU_WIDTH // HEAD_DIM
CONV_W = 4
LRU_C = 8.0
ATT_HEADS = (D_MODEL // 2) // HEAD_DIM
ATT_KV_HEADS = ATT_HEADS // 4
WINDOW = 128
BLOCK = 128
ROPE_THETA = 10000.0
NA_HEADS = D_MODEL // HEAD_DIM
NA_D = NA_HEADS * HEAD_DIM
NA_KH = 8
NA_KW = 16
NA_COL_BLOCK = 16
NA_COL_SPAN = 2 * NA_KW
N_EVEN = (DEPTH + 1) // 2
N_ODD = DEPTH // 2
AB_IN = 2 * LRU_WIDTH + (ATT_HEADS + 2 * ATT_KV_HEADS) * HEAD_DIM
ATTN_SCALE = HEAD_DIM ** -0.5

kernel_name = "hybrid_rglru_swa_natten_prefix_dit"


def rmsnorm(x, g):
    x32 = x.astype(jnp.float32)
    y = x32 * lax.rsqrt(jnp.mean(x32 * x32, axis=-1, keepdims=True) + EPS)
    return (y * g).astype(x.dtype)


def modulate(h, shift, scale):
    return h * (1.0 + scale) + shift


def half_ffn(x, g, shift, scale, gate, w_in, w_out):
    h = modulate(rmsnorm(x, g), shift, scale)
    gt, up = jnp.split(h @ w_in, 2, axis=-1)
    return x + 0.5 * gate * ((jax.nn.silu(gt) * up) @ w_out)


def heads(t, n):
    return t.reshape(*t.shape[:-1], n, HEAD_DIM)


def axial_rope(x, pos_row, pos_col):
    half = HEAD_DIM // 2
    nf = half // 2
    inv = ROPE_THETA ** (-jnp.arange(nf, dtype=jnp.float32) / nf)

    def rot(xa, pos):
        ang = pos.astype(jnp.float32)[:, None] * inv[None, :]
        cos = jnp.cos(ang)[None, :, None, :]
        sin = jnp.sin(ang)[None, :, None, :]
        x1, x2 = xa[..., :nf], xa[..., nf:]
        return jnp.concatenate([x1 * cos - x2 * sin, x2 * cos + x1 * sin], axis=-1)

    return jnp.concatenate([rot(x[..., :half], pos_row), rot(x[..., half:], pos_col)],
                           axis=-1).astype(x.dtype)


def centred_depthwise_conv(u, w, b):
    L = u.shape[1]
    left = CONV_W // 2
    up = jnp.pad(u, ((0, 0), (left, CONV_W - 1 - left), (0, 0)))
    out = b
    for k in range(CONV_W):
        out = out + up[:, k:k + L] * w[k]
    return out


def rglru_coeffs(u, w_a, b_a, w_x, b_x, lam):
    ub = u.reshape(*u.shape[:-1], LRU_BLOCKS, HEAD_DIM)
    gate_r = jnp.einsum('blnc,ncd->blnd', ub, w_a).reshape(u.shape) + b_a
    gate_i = jnp.einsum('blnc,ncd->blnd', ub, w_x).reshape(u.shape) + b_x
    r = jax.nn.sigmoid(gate_r.astype(jnp.float32))
    i = jax.nn.sigmoid(gate_i.astype(jnp.float32))
    log_a = -LRU_C * r * jax.nn.softplus(-lam.astype(jnp.float32))
    a = jnp.exp(log_a)
    b = jnp.sqrt(-jnp.expm1(2.0 * log_a)) * (i * u.astype(jnp.float32))
    return a, b


def linear_scan(a, b, h0, reverse):
    def combine(e1, e2):
        a1, b1 = e1
        a2, b2 = e2
        return a1 * a2, a2 * b1 + b2

    a_cum, h = lax.associative_scan(combine, (a, b), axis=1, reverse=reverse)
    if h0 is not None:
        h = h + a_cum * h0[:, None]
    return h


def with_sink(s, sink):
    g = ATT_HEADS // ATT_KV_HEADS
    col = jnp.broadcast_to(sink.astype(jnp.float32).reshape(ATT_KV_HEADS, g, 1, 1),
                           s.shape[:-1] + (1,))
    return jnp.concatenate([s, col], axis=-1)


def windowed_gqa(q, k, v, kc, vc, sink):
    B, L, H, hd = q.shape
    nb = L // BLOCK
    g = H // ATT_KV_HEADS
    n_ctx = kc.shape[1]
    qb = q.reshape(B, nb, BLOCK, ATT_KV_HEADS, g, hd)

    def band(t):
        tp = jnp.pad(t, ((0, 0), (BLOCK, BLOCK), (0, 0), (0, 0)))
        tp = tp.reshape(B, nb + 2, BLOCK, ATT_KV_HEADS, hd)
        return jnp.concatenate([tp[:, :-2], tp[:, 1:-1], tp[:, 2:]], axis=2)

    kw, vw = band(k), band(v)
    nk = 3 * BLOCK
    s_loc = jnp.einsum('bnqkgd,bnskd->bnkgqs', qb, kw).astype(jnp.float32) * ATTN_SCALE
    qpos = np.arange(nb)[:, None] * BLOCK + np.arange(BLOCK)[None, :]
    kpos = (np.arange(nb)[:, None] - 1) * BLOCK + np.arange(nk)[None, :]
    valid = ((np.abs(qpos[:, :, None] - kpos[:, None, :]) <= WINDOW)
             & (kpos[:, None, :] >= 0) & (kpos[:, None, :] < L))
    s_loc = jnp.where(valid[None, :, None, None], s_loc, -jnp.inf)
    s_ctx = jnp.einsum('bnqkgd,bskd->bnkgqs', qb, kc).astype(jnp.float32) * ATTN_SCALE
    p = jax.nn.softmax(with_sink(jnp.concatenate([s_loc, s_ctx], axis=-1), sink), axis=-1)
    o = (jnp.einsum('bnkgqs,bnskd->bnqkgd', p[..., :nk].astype(v.dtype), vw)
         + jnp.einsum('bnkgqs,bskd->bnqkgd', p[..., nk:nk + n_ctx].astype(vc.dtype), vc))
    return o.reshape(B, L, H * hd)


def context_gqa(qc, kc, vc, sink):
    B, C, H, hd = qc.shape
    g = H // ATT_KV_HEADS
    qg = qc.reshape(B, C, ATT_KV_HEADS, g, hd)
    s = jnp.einsum('bqkgd,bskd->bkgqs', qg, kc).astype(jnp.float32) * ATTN_SCALE
    p = jax.nn.softmax(with_sink(s, sink), axis=-1)[..., :C]
    o = jnp.einsum('bkgqs,bskd->bqkgd', p.astype(vc.dtype), vc)
    return o.reshape(B, C, H * hd)


def mixer_ab(h_lat, h_ctx, w_in, conv_w, conv_b, w_a, b_a, w_x, b_x, lam,
             q_g, k_g, sink, w_out, pos_row, pos_col, ctx_out):
    splits = np.cumsum([LRU_WIDTH, LRU_WIDTH, ATT_HEADS * HEAD_DIM, ATT_KV_HEADS * HEAD_DIM]).tolist()
    xl_lat, gl_lat, q_lat, k_lat, v_lat = jnp.split(h_lat @ w_in, splits, axis=-1)
    if ctx_out:
        xl_ctx, gl_ctx, q_ctx, k_ctx, v_ctx = jnp.split(h_ctx @ w_in, splits, axis=-1)
    else:
        w_ctx = jnp.concatenate([w_in[:, :LRU_WIDTH], w_in[:, splits[2]:]], axis=1)
        xl_ctx, k_ctx, v_ctx = jnp.split(h_ctx @ w_ctx, [LRU_WIDTH, LRU_WIDTH + ATT_KV_HEADS * HEAD_DIM], axis=-1)

    u_lat = centred_depthwise_conv(xl_lat, conv_w, conv_b)
    u_ctx = centred_depthwise_conv(xl_ctx, conv_w, conv_b)
    h_lat_sum, h_ctx_sum = None, None
    for d, rev in enumerate((False, True)):
        a_c, b_c = rglru_coeffs(u_ctx, w_a[d], b_a[d], w_x[d], b_x[d], lam[d])
        hc = linear_scan(a_c, b_c, None, rev)
        h0 = hc[:, 0] if rev else hc[:, -1]
        a_l, b_l = rglru_coeffs(u_lat, w_a[d], b_a[d], w_x[d], b_x[d], lam[d])
        hl = linear_scan(a_l, b_l, h0, rev)
        h_lat_sum = hl if h_lat_sum is None else h_lat_sum + hl
        h_ctx_sum = hc if h_ctx_sum is None else h_ctx_sum + hc
    lru_lat = (h_lat_sum * jax.nn.gelu(gl_lat.astype(jnp.float32))).astype(h_lat.dtype)

    q_lat = axial_rope(rmsnorm(heads(q_lat, ATT_HEADS), q_g), pos_row, pos_col)
    k_lat = axial_rope(rmsnorm(heads(k_lat, ATT_KV_HEADS), k_g), pos_row, pos_col)
    v_lat = heads(v_lat, ATT_KV_HEADS)
    k_ctx = rmsnorm(heads(k_ctx, ATT_KV_HEADS), k_g)
    v_ctx = heads(v_ctx, ATT_KV_HEADS)
    att_lat = windowed_gqa(q_lat, k_lat, v_lat, k_ctx, v_ctx, sink)
    y_lat = jnp.concatenate([lru_lat, att_lat], axis=-1) @ w_out
    y_ctx = None
    if ctx_out:
        lru_ctx = (h_ctx_sum * jax.nn.gelu(gl_ctx.astype(jnp.float32))).astype(h_ctx.dtype)
        att_ctx = context_gqa(rmsnorm(heads(q_ctx, ATT_HEADS), q_g), k_ctx, v_ctx, sink)
        y_ctx = jnp.concatenate([lru_ctx, att_ctx], axis=-1) @ w_out
    return y_lat, y_ctx


def mixer_na(h_lat, h_ctx, w_in, q_g, k_g, rpb, w_out, ctx_out):
    B, L, _ = h_lat.shape
    rows = L // GRID_W
    kh = min(NA_KH, rows)
    q, k, v = jnp.split(h_lat @ w_in, 3, axis=-1)
    q = rmsnorm(heads(q, NA_HEADS), q_g)
    k = rmsnorm(heads(k, NA_HEADS), k_g)
    v = heads(v, NA_HEADS)
    if ctx_out:
        qc, kc, vc = jnp.split(h_ctx @ w_in, 3, axis=-1)
    else:
        kc, vc = jnp.split(h_ctx @ w_in[:, NA_D:], 2, axis=-1)
    kc = rmsnorm(heads(kc, NA_HEADS), k_g)
    vc = heads(vc, NA_HEADS)
    n_ctx = kc.shape[1]

    qg = q.reshape(B, rows, GRID_W, NA_HEADS, HEAD_DIM)
    kg = k.reshape(B, rows, GRID_W, NA_HEADS, HEAD_DIM)
    vg = v.reshape(B, rows, GRID_W, NA_HEADS, HEAD_DIM)

    ncb = GRID_W // NA_COL_BLOCK
    m_idx = np.arange(ncb)
    col_start = np.clip(m_idx * NA_COL_BLOCK - NA_KW // 2, 0, GRID_W - NA_COL_SPAN)
    key_col = col_start[:, None] + np.arange(NA_COL_SPAN)[None, :]
    q_col = m_idx[:, None] * NA_COL_BLOCK + np.arange(NA_COL_BLOCK)[None, :]
    win_start = np.clip(q_col - NA_KW // 2, 0, GRID_W - NA_KW)
    kcol = key_col[:, None, :]
    col_valid = (kcol >= win_start[..., None]) & (kcol < win_start[..., None] + NA_KW)
    col_off = np.clip(kcol - q_col[..., None] + NA_KW - 1, 0, 2 * NA_KW - 2)
    n_loc = kh * NA_COL_SPAN

    def row_step(r):
        rs = jnp.clip(r - kh // 2, 0, rows - kh)
        qr = lax.dynamic_index_in_dim(qg, r, axis=1, keepdims=False)
        kr = lax.dynamic_slice_in_dim(kg, rs, kh, axis=1)
        vr = lax.dynamic_slice_in_dim(vg, rs, kh, axis=1)
        kb = kr[:, :, key_col]
        vb = vr[:, :, key_col]
        qb = qr.reshape(B, ncb, NA_COL_BLOCK, NA_HEADS, HEAD_DIM)
        s_loc = jnp.einsum('bmqhd,bimshd->bhmqis', qb, kb).astype(jnp.float32) * ATTN_SCALE
        ro = rs + jnp.arange(kh) - r + (NA_KH - 1)
        bias = rpb[:, ro[:, None, None, None], col_off[None]]
        bias = jnp.transpose(bias, (0, 2, 3, 1, 4)).astype(jnp.float32)
        s_loc = jnp.where(col_valid[:, :, None, :], s_loc + bias, -jnp.inf)
        s_loc = s_loc.reshape(B, NA_HEADS, ncb, NA_COL_BLOCK, n_loc)
        s_ctx = jnp.einsum('bmqhd,bshd->bhmqs', qb, kc).astype(jnp.float32) * ATTN_SCALE
        p = jax.nn.softmax(jnp.concatenate([s_loc, s_ctx], axis=-1), axis=-1)
        p_loc = p[..., :n_loc].reshape(B, NA_HEADS, ncb, NA_COL_BLOCK, kh, NA_COL_SPAN)
        o = (jnp.einsum('bhmqis,bimshd->bmqhd', p_loc.astype(v.dtype), vb)
             + jnp.einsum('bhmqs,bshd->bmqhd', p[..., n_loc:].astype(vc.dtype), vc))
        return o.reshape(B, GRID_W, NA_D)

    out = lax.map(row_step, jnp.arange(rows, dtype=jnp.int32))
    y_lat = jnp.moveaxis(out, 0, 1).reshape(B, L, NA_D) @ w_out
    y_ctx = None
    if ctx_out:
        qc = rmsnorm(heads(qc, NA_HEADS), q_g)
        s = jnp.einsum('bqhd,bshd->bhqs', qc, kc).astype(jnp.float32) * ATTN_SCALE
        p = jax.nn.softmax(s, axis=-1)
        oc = jnp.einsum('bhqs,bshd->bqhd', p.astype(vc.dtype), vc)
        y_ctx = oc.reshape(B, n_ctx, NA_D) @ w_out
    return y_lat, y_ctx


def setup_inputs(seed: int = 0) -> dict:
    key = jax.random.key(seed)
    ks = iter(jax.random.split(key, 32))

    def nrm(shape, scale):
        return jax.random.normal(next(ks), shape, jnp.float32) * scale

    D = D_MODEL
    a_init = jax.random.uniform(next(ks), (N_EVEN, 2, LRU_WIDTH), jnp.float32, minval=0.9, maxval=0.999)
    s_init = a_init ** (1.0 / LRU_C)
    lru_lambda = jnp.log(s_init) - jnp.log1p(-s_init)
    return {
        "x": nrm((BATCH, SEQ, D), 1.0),
        "c": nrm((BATCH, D), 1.0),
        "ctx": nrm((BATCH, CTX_LEN, D), 1.0),
        "c_ctx": nrm((D,), 1.0),
        "w_mod": nrm((DEPTH, D, N_MOD * D), 0.5 * D ** -0.5),
        "b_mod": nrm((DEPTH, N_MOD * D), 0.01),
        "norm_g": 1.0 + nrm((DEPTH, 3, D), 0.05),
        "ffn_w_in": nrm((DEPTH, 2, D, 2 * D_FF), D ** -0.5),
        "ffn_w_out": nrm((DEPTH, 2, D_FF, D), D_FF ** -0.5),
        "ab_w_in": nrm((N_EVEN, D, AB_IN), D ** -0.5),
        "lru_conv_w": nrm((N_EVEN, CONV_W, LRU_WIDTH), CONV_W ** -0.5),
        "lru_conv_b": nrm((N_EVEN, LRU_WIDTH), 0.01),
        "lru_w_a": nrm((N_EVEN, 2, LRU_BLOCKS, HEAD_DIM, HEAD_DIM), HEAD_DIM ** -0.5),
        "lru_b_a": nrm((N_EVEN, 2, LRU_WIDTH), 0.01),
        "lru_w_x": nrm((N_EVEN, 2, LRU_BLOCKS, HEAD_DIM, HEAD_DIM), HEAD_DIM ** -0.5),
        "lru_b_x": nrm((N_EVEN, 2, LRU_WIDTH), 0.01),
        "lru_lambda": lru_lambda,
        "attn_q_norm": 1.0 + nrm((N_EVEN, HEAD_DIM), 0.05),
        "attn_k_norm": 1.0 + nrm((N_EVEN, HEAD_DIM), 0.05),
        "attn_sink": nrm((N_EVEN, ATT_HEADS), 0.5),
        "ab_w_out": nrm((N_EVEN, LRU_WIDTH + ATT_HEADS * HEAD_DIM, D), (LRU_WIDTH + ATT_HEADS * HEAD_DIM) ** -0.5),
        "na_w_in": nrm((N_ODD, D, 3 * NA_D), D ** -0.5),
        "na_q_norm": 1.0 + nrm((N_ODD, HEAD_DIM), 0.05),
        "na_k_norm": 1.0 + nrm((N_ODD, HEAD_DIM), 0.05),
        "na_rpb": nrm((N_ODD, NA_HEADS, 2 * NA_KH - 1, 2 * NA_KW - 1), 0.1),
        "na_w_out": nrm((N_ODD, NA_D, D), NA_D ** -0.5),
    }


def reference(x, c, ctx, c_ctx, w_mod, b_mod, norm_g, ffn_w_in, ffn_w_out,
              ab_w_in, lru_conv_w, lru_conv_b, lru_w_a, lru_b_a, lru_w_x, lru_b_x, lru_lambda,
              attn_q_norm, attn_k_norm, attn_sink, ab_w_out,
              na_w_in, na_q_norm, na_k_norm, na_rpb, na_w_out):
    B, L, D = x.shape
    pos = jnp.arange(L, dtype=jnp.int32)
    pos_row, pos_col = pos // GRID_W, pos % GRID_W
    x_lat, x_ctx = x, ctx
    for l in range(DEPTH):
        ctx_out = l < DEPTH - 1
        m_lat = (jax.nn.silu(c) @ w_mod[l] + b_mod[l]).reshape(B, N_MOD, D)[:, :, None, :]
        m_ctx = (jax.nn.silu(c_ctx) @ w_mod[l] + b_mod[l]).reshape(N_MOD, D)
        x_lat = half_ffn(x_lat, norm_g[l, 0], m_lat[:, 0], m_lat[:, 1], m_lat[:, 2], ffn_w_in[l, 0], ffn_w_out[l, 0])
        x_ctx = half_ffn(x_ctx, norm_g[l, 0], m_ctx[0], m_ctx[1], m_ctx[2], ffn_w_in[l, 0], ffn_w_out[l, 0])
        h_lat = modulate(rmsnorm(x_lat, norm_g[l, 1]), m_lat[:, 3], m_lat[:, 4])
        h_ctx = modulate(rmsnorm(x_ctx, norm_g[l, 1]), m_ctx[3], m_ctx[4])
        if l % 2 == 0:
            e = l // 2
            y_lat, y_ctx = mixer_ab(h_lat, h_ctx, ab_w_in[e], lru_conv_w[e], lru_conv_b[e],
                                    lru_w_a[e], lru_b_a[e], lru_w_x[e], lru_b_x[e], lru_lambda[e],
                                    attn_q_norm[e], attn_k_norm[e], attn_sink[e], ab_w_out[e],
                                    pos_row, pos_col, ctx_out)
        else:
            o = l // 2
            y_lat, y_ctx = mixer_na(h_lat, h_ctx, na_w_in[o], na_q_norm[o], na_k_norm[o],
                                    na_rpb[o], na_w_out[o], ctx_out)
        x_lat = x_lat + m_lat[:, 5] * y_lat
        x_lat = half_ffn(x_lat, norm_g[l, 2], m_lat[:, 6], m_lat[:, 7], m_lat[:, 8], ffn_w_in[l, 1], ffn_w_out[l, 1])
        if ctx_out:
            x_ctx = x_ctx + m_ctx[5] * y_ctx
            x_ctx = half_ffn(x_ctx, norm_g[l, 2], m_ctx[6], m_ctx[7], m_ctx[8], ffn_w_in[l, 1], ffn_w_out[l, 1])
    return x_lat
```

```python
from contextlib import ExitStack
import numpy as np
import concourse.bass as bass
import concourse.mybir as mybir
from concourse.bass_utils import run_bass_kernel_spmd

F32 = mybir.dt.float32
BF16 = mybir.dt.bfloat16
AF = mybir.ActivationFunctionType
ALU = mybir.AluOpType

D = 2048
KC = 16
DFF = 5504
FC = 43
SEQ = 4096
LAT = 4096
CTX = 256
HALF = True
MIRROR_TEST = True
NCORES = 8 if HALF else 4
OWN = 2048 if HALF else 4096
NCH_ALL = 8
NCH_EXT = 5 if HALF else 8
NCH_OWN = OWN // 512
EPS = 1e-6
BIG = 30000.0
NT = LAT + CTX
NSP = 108
SP_CW, SP_CB, SP_BA, SP_BX, SP_LM, SP_QG, SP_KG, SP_SK, SP_NQG, SP_NKG = 0, 40, 48, 64, 80, 96, 97, 98, 106, 107
C_SP, C_MASK, C_SP8, C_GQS, C_ESINK, C_NQS = 0, 108, 1132, 1148, 1149, 1157


class Tk:
    __slots__ = ("name", "w", "r")

    def __init__(self, name=""):
        self.name = name
        self.w = []
        self.r = {}


class Stream:
    def __init__(self, name, sem):
        self.name = name
        self.sem = sem
        self.items = []
        self.n = 0
        self.pending = False
        self.seen = {}


class Prog:
    NP = 12

    def __init__(self, nc, es):
        self.nc = nc
        self.streams = {n: Stream(n, es.enter_context(nc.semaphore("s_" + n)))
                        for n in ("pe", "act", "dve", "pool", "sp")}
        self.dpool = {n: [es.enter_context(nc.semaphore(f"d_{n}{i}")) for i in range(np_)]
                      for n, np_ in (("sp", self.NP), ("act", self.NP), ("pool", self.NP), ("pst", self.NP), ("cast", 5))}
        self.dnext = {n: 0 for n in self.dpool}
        self.dtot = {}
        self.phase_tiles = []
        self.es = es
        self.ncc = 0
        self._scopes = {}
        self._scope = None

    def tk_scope(self, key):
        if key is None:
            self._scope = None
            return
        self._scope = self._scopes.setdefault(key, [])
        self._scope_i = 0

    def tk(self, name="", phase=True):
        if getattr(self, "_scope", None) is not None and phase:
            if self._scope_i < len(self._scope):
                t = self._scope[self._scope_i]
            else:
                t = Tk(name)
                self._scope.append(t)
                self.phase_tiles.append(t)
            self._scope_i += 1
            return t
        t = Tk(name)
        if phase:
            self.phase_tiles.append(t)
        return t

    def _wait(self, S, ev):
        kind, key, val = ev
        if kind == "eng":
            if key == S.name:
                if key == "pe" or val <= S.n - 3:
                    return
            k = key
            sem = self.streams[key].sem
        else:
            k = id(key)
            sem = key
        if S.seen.get(k, 0) >= val:
            return
        S.seen[k] = val
        S.items.append(("wait", sem, val))

    def _deps(self, S, reads, writes, partial):
        for t in reads:
            for ev in t.w:
                self._wait(S, ev)
        for t in writes:
            if not partial:
                for ev in t.w:
                    self._wait(S, ev)
            for ev in t.r.values():
                self._wait(S, ev)

    def op(self, sname, fn, reads=(), writes=(), ms=True):
        S = self.streams[sname]
        self._deps(S, reads, writes, False)
        if ms:
            S.n += 1
            tick = S.n
            S.items.append(("op", fn, S.sem, 1))
        else:
            tick = S.n + 1
            S.items.append(("op", fn, None, 0))
        S.pending = not ms
        ev = ("eng", sname, tick)
        for t in writes:
            t.w = [ev]
            t.r = {}
        for t in reads:
            t.r[sname] = ev

    def dma(self, sname, out, in_, reads=(), writes=(), partial=False, pn=None):
        S = self.streams[sname]
        self._deps(S, reads, writes, partial)
        pn = pn or sname
        pool = self.dpool[pn]
        i = self.dnext[pn]
        self.dnext[pn] = (i + 1) % len(pool)
        sem = pool[i]
        tot = self.dtot.get(id(sem), 0)
        if tot:
            self._wait(S, ("dma", sem, tot))
        tot += 16
        self.dtot[id(sem)] = tot
        S.items.append(("op", lambda e, out=out, in_=in_: e.dma_start(out=out, in_=in_), sem, 16))
        ev = ("dma", sem, tot)
        for t in writes:
            if partial:
                t.w.append(ev)
            else:
                t.w = [ev]
                t.r = {}
        for t in reads:
            t.r[("d", id(sem))] = ev

    def allgather(self, groups, src, dst, reads=(), writes=()):
        S = self.streams["pool"]
        self._deps(S, reads, writes, False)
        sem = self.es.enter_context(self.nc.semaphore(f"cc{self.ncc}"))
        self.ncc += 1
        S.items.append(("op", lambda e: e.collective_compute(
            "AllGather", ALU.bypass, replica_groups=groups, ins=[src], outs=[dst]), sem, -1))
        ev = ("dma", sem, 1)
        for t in writes:
            t.w = [ev]
            t.r = {}
        for t in reads:
            t.r[("d", id(sem))] = ev

    def mm(self, out, lhsT, rhs, start, stop, reads, writes, ms=None):
        self.op("pe", lambda e: e.matmul(out, lhsT, rhs, start=start, stop=stop),
                reads=reads, writes=writes, ms=stop if ms is None else ms)

    def barrier(self):
        names = ("pe", "act", "dve", "sp")
        evs = []
        for n in ("pe", "act", "dve"):
            S = self.streams[n]
            assert not S.pending, n
            if S.n:
                evs.append(("eng", n, S.n))
        for n in ("sp", "act", "pst"):
            for sem in self.dpool[n]:
                tot = self.dtot.get(id(sem), 0)
                if tot:
                    evs.append(("dma", sem, tot))
        for n in names:
            for ev in evs:
                self._wait(self.streams[n], ev)
        for t in self.phase_tiles:
            t.w = []
            t.r = {}
        self.phase_tiles = []

    @staticmethod
    def _run(items, e):
        for it in items:
            if it[0] == "wait":
                e.wait_ge(it[1], it[2])
            else:
                ins = it[1](e)
                if it[2] is not None:
                    if it[3] == 1 and False:
                        ins.then_inc(it[2])
                    else:
                        ins.then_inc(it[2], it[3]) if it[3] != -1 else ins.then_inc(it[2])

    def check_deadlock(self):
        val = {}
        pc = {n: 0 for n in self.streams}
        progress = True
        while progress:
            progress = False
            for n, S in self.streams.items():
                while pc[n] < len(S.items):
                    it = S.items[pc[n]]
                    if it[0] == "wait":
                        if val.get(id(it[1]), 0) >= it[2]:
                            pc[n] += 1; progress = True
                        else:
                            break
                    else:
                        if it[2] is not None:
                            val[id(it[2])] = val.get(id(it[2]), 0) + abs(it[3])
                        pc[n] += 1; progress = True
        bad = {n: (pc[n], len(S.items)) for n, S in self.streams.items() if pc[n] < len(S.items)}
        if bad:
            for n in bad:
                it = self.streams[n].items[pc[n]]
                print("DEADLOCK", n, bad[n], it[0], getattr(it[1], "name", it[1]), it[2], "cur", val.get(id(it[1]), 0))
        return not bad

    def emit(self):
        nc = self.nc
        assert self.check_deadlock()
        print("stream lens", {n: len(S.items) for n, S in self.streams.items()})
        with nc.Block() as block:
            block.tensor(lambda e: self._run(self.streams["pe"].items, e))
            block.scalar(lambda e: self._run(self.streams["act"].items, e))
            block.vector(lambda e: self._run(self.streams["dve"].items, e))
            block.gpsimd(lambda e: self._run(self.streams["pool"].items, e))
            block.sync(lambda e: self._run(self.streams["sp"].items, e))


class Arena:
    def __init__(self, t, size):
        self.t = t
        self.size = size
        self.off = 0

    def reset(self):
        self.off = 0

    def take(self, n):
        assert self.off + n <= self.size, (self.off, n, self.size)
        a = self.t[:, self.off:self.off + n]
        self.off += n
        return a


WSPEC = [
    ("f_in_00", D, 2 * DFF), ("f_out_00", DFF, D),
    ("ab_in", D, 3584), ("ab_out", D, D),
    ("f_in_01", D, 2 * DFF), ("f_out_01", DFF, D),
    ("f_in_10", D, 2 * DFF), ("f_out_10", DFF, D),
    ("na_in", D, 6144), ("na_out", D, D),
    ("f_in_11", D, 2 * DFF), ("f_out_11", DFF, D),
]


class Builder:
    def __init__(self, stop_after=None, debug=()):
        self.stop_after = stop_after
        self.debug = debug
        self.nc = bass.Bass("TRN2", target_bir_lowering=False)

    def ext_in(self, name, shape, dt=F32):
        return self.nc.dram_tensor(name, list(shape), dt, kind="ExternalInput").ap()

    def ext_out(self, name, shape, dt=F32):
        return self.nc.dram_tensor(name, list(shape), dt, kind="ExternalOutput").ap()

    def dram(self, name, shape, dt=F32):
        return self.nc.dram_tensor(name, list(shape), dt).ap()

    def build(self):
        nc = self.nc
        with ExitStack() as es:
            self.es = es
            P = self.P = Prog(nc, es)
            sb = lambda name, shape, dt: es.enter_context(nc.sbuf_tensor(name, list(shape), dt))
            self.ring16 = sb("ring16", [128, 6, KC * 128], BF16)
            self.ring16_t = [P.tk(f"r16_{i}", phase=False) for i in range(6)]
            self.ring43 = sb("ring43", [128, 2, FC * 128], BF16)
            self.ring43_t = [P.tk(f"r43_{i}", phase=False) for i in range(2)]
            self.r16n = 0
            self.r43n = 0
            self.ones32 = sb("ones32", [128, 128], F32)
            self.ones16 = sb("ones16", [128, 128], BF16)
            self.epsc = sb("epsc", [128, 2], F32)
            self.modtab = sb("modtab", [128, 2, 2, 9 * KC], F32)
            self.consts_t = P.tk("consts", phase=False)
            a16 = sb("a16", [128, 38400], BF16)
            a32 = sb("a32", [128, 16000], F32)
            self.A16 = Arena(a16, 38400)
            self.A32 = Arena(a32, 16000)
            self.cst32 = sb("cst32", [128, 1200], F32)
            self.modc32 = sb("modc32", [128, 1024], F32)
            self.modc16 = sb("modc16", [128, 32], BF16)
            self.cst16 = sb("cst16", [128, 256 + 4096], BF16)
            self.psum = [es.enter_context(nc.psum_tensor(f"ps{i}", [128, 512], F32)) for i in range(8)]
            self.psum_t = [P.tk(f"ps{i}", phase=False) for i in range(8)]

            self.declare_io()
            self.init_consts()
            self.prep_weights()
            self.P.barrier()
            self.body()
            P.barrier()
            P.emit()
        return nc

    def declare_io(self):
        self.xT = self.ext_in("xT", [D, LAT])
        self.ctxT = self.ext_in("ctxT", [D, CTX])
        self.cT = self.ext_in("cT", [128, KC * 2])
        self.wmod = self.ext_in("wmod", [2, D, 9 * D])
        self.bmod = self.ext_in("bmod", [128, 2 * 144])
        self.normg = self.ext_in("normg", [128, 2 * 3 * KC])
        self.wext = {}
        for name, rows, cols in WSPEC:
            if self.stop_after in ("ffn00", "mod", "prep") and name not in ("f_in_00", "f_out_00"):
                continue
            if self.stop_after == "mix0" and name not in ("f_in_00", "f_out_00", "ab_in", "ab_out"):
                continue
            self.wext[name] = self.ext_in("w_" + name, [rows, cols])
        full = self.stop_after not in ("ffn00", "mod", "prep")
        self.full = full
        if full:
            self.ropeT = self.ext_in("ropeT", [128, 2 * LAT])
            self.prot = self.ext_in("prot", [128, 256])
            self.maskPN = self.ext_in("maskPN", [128, 1024])
            self.smallp = self.ext_in("smallp", [128, NSP])
            self.gatew = self.ext_in("gatew", [128, 4096])
            mk = self.ext_out if "dbg" in self.debug else self.dram
            self.XL = mk("XL", [1024, NT])
            self.GG = mk("GG", [1024, NT])
            self.HP = mk("HP", [1024, NT])
            self.QT = mk("QT", [1024, NT], BF16)
            self.KT = mk("KT", [256, NT], BF16)
            self.VV = mk("VV", [NT, 256], BF16)
            self.CAT = mk("CAT", [D, NT], BF16)
            self.natt = self.ext_in("natt", [128, 16 * 960])
            self.namask = self.ext_in("namask", [128, 384])
            self.QT1 = self.dram("QT1", [D, NT], BF16)
            self.KT1 = self.dram("KT1", [D, NT], BF16)
            self.VV1 = self.dram("VV1", [NT, D], BF16)
        self.outT = self.ext_out("outT", [D, OWN])
        self.XT = self.dram("XT", [D, LAT + CTX])
        self.XT_t = {}

    def xt_tk(self, c):
        if c not in self.XT_t or self.XT_t[c] not in self.P.phase_tiles:
            self.XT_t[c] = self.P.tk(f"XT{c}")
        return self.XT_t[c]

    def prep_B(self, name):
        P = self.P
        rows, cols = self.wdims[name]
        kc = rows // 128
        nt = cols // 128
        full = self.wext[name]
        w16 = self.dram("wb_" + name, [nt, 128, kc * 128], BF16)
        fv = full.rearrange("(kc p) n -> p kc n", p=128)
        tks = []
        for n in range(nt):
            t = P.tk(phase=False)
            P.dma("pool", w16[n].rearrange("p (kc n) -> p kc n", n=128), fv[:, :, n * 128:(n + 1) * 128], writes=[t], pn="cast")
            tks.append(t)
        self.w16[name] = w16
        self.w16_t[name] = tks

    def prep_weights(self):
        self.w16 = {}
        self.w16_t = {}
        self.wdims = {n: (r, c) for n, r, c in WSPEC}
        names = [n for n, _, _ in WSPEC]
        if self.stop_after in ("ffn00", "mod", "prep"):
            names = names[:2]
        if self.stop_after == "mix0":
            names = names[:4]
        if self.stop_after != "prep":
            self.modulation()
        for n in names:
            self.prep_B(n)

    def wtile(self, name, n):
        P = self.P
        w16 = self.w16[name]
        kc = w16.shape[2] // 128
        if kc == KC:
            i = self.r16n % 6
            self.r16n += 1
            slot, t = self.ring16[:, i, :], self.ring16_t[i]
        else:
            i = self.r43n % 2
            self.r43n += 1
            slot, t = self.ring43[:, i, :], self.ring43_t[i]
        P.dma("sp", slot, w16[n], reads=[self.w16_t[name][n]], writes=[t])
        return slot.rearrange("p (kc n) -> p kc n", n=128), t

    def init_consts(self):
        P = self.P
        P.op("dve", lambda e: e.memset(self.ones32[:], 1.0), writes=[self.consts_t])
        P.op("dve", lambda e: e.memset(self.ones16[:], 1.0), writes=[self.consts_t])
        P.op("dve", lambda e: e.memset(self.epsc[:], EPS), writes=[self.consts_t])

    def mod_setup(self):
        P = self.P
        mc = self.modc32
        self.m_c32 = mc[:, 0:32]; self.m_sig = mc[:, 32:64]; self.m_bm = mc[:, 64:352]; self.m_ng = mc[:, 352:448]; self.m_msel = mc[:, 448:736]
        self.m_sc16 = self.modc16[:, 0:32]
        self.t_mc = P.tk(phase=False); self.t_sc = P.tk(phase=False); self.t_msel = P.tk(phase=False)
        P.dma("sp", self.m_c32, self.cT[:, :], writes=[self.t_mc])
        P.dma("sp", self.m_bm, self.bmod[:, :], writes=[self.t_mc], partial=True)
        P.dma("sp", self.m_ng, self.normg[:, :], writes=[self.t_mc], partial=True)
        P.op("act", lambda e: e.activation(out=self.m_sig, in_=self.m_c32, func=AF.Sigmoid), reads=[self.t_mc], writes=[self.t_sc])
        P.op("dve", lambda e: e.tensor_tensor(out=self.m_sc16, in0=self.m_sig, in1=self.m_c32, op=ALU.mult), reads=[self.t_sc, self.t_mc], writes=[self.t_sc])

    def mod_tiles(self, l, gcs, bank=0):
        P = self.P
        sc3 = self.m_sc16.rearrange("p (kc r) -> p kc r", r=2)
        ps, pst = self.psum[bank], self.psum_t[bank]
        for gc in gcs:
            i = self.r16n % 6
            self.r16n += 1
            slot, t = self.ring16[:, i, :], self.ring16_t[i]
            src = self.wmod[l].rearrange("(kc p) n -> p kc n", p=128)[:, :, gc * 128:(gc + 1) * 128]
            s3 = slot.rearrange("p (kc n) -> p kc n", n=128)
            P.dma("pool", s3, src, writes=[t])
            for kc in range(KC):
                P.mm(ps[:, 2 * gc:2 * gc + 2], s3[:, kc, :], sc3[:, kc, :], kc == 0, kc == KC - 1, reads=[t, self.t_sc], writes=[pst])

    def mod_finish(self, l, bank=0, subs=(0, 1, 2)):
        P = self.P
        ps, pst = self.psum[bank], self.psum_t[bank]
        ms4 = self.m_msel.rearrange("p (g w) -> p g w", g=144, w=2)
        bm3 = self.m_bm.rearrange("p (l g) -> p l g", l=2)
        ng4 = self.m_ng.rearrange("p (l s k) -> p l s k", l=2, s=3)
        t_msel, t_mc = self.t_msel, self.t_mc
        for w in range(2):
            P.op("dve", lambda e, d=ms4[:, :, w], p_=ps[:, 0:288].rearrange("p (g w) -> p g w", w=2)[:, :, w], b=bm3[:, l, :]:
                 e.tensor_tensor(out=d, in0=p_, in1=b, op=ALU.add), reads=[pst, t_mc], writes=[t_msel])
            for s in subs:
                shift = ms4[:, (3 * s) * KC:(3 * s + 1) * KC, w]
                scale = ms4[:, (3 * s + 1) * KC:(3 * s + 2) * KC, w]
                gate = ms4[:, (3 * s + 2) * KC:(3 * s + 3) * KC, w]
                Ad = self.modtab[:, l, w, (3 * s) * KC:(3 * s + 1) * KC]
                Bd = self.modtab[:, l, w, (3 * s + 1) * KC:(3 * s + 2) * KC]
                Gd = self.modtab[:, l, w, (3 * s + 2) * KC:(3 * s + 3) * KC]
                P.op("dve", lambda e, d=Ad, sc=scale, g=ng4[:, l, s, :]: e.scalar_tensor_tensor(
                    out=d, in0=sc, scalar=1.0, in1=g, op0=ALU.add, op1=ALU.mult), reads=[t_msel, t_mc], writes=[self.consts_t])
                P.op("dve", lambda e, d=Bd, sh=shift: e.tensor_copy(out=d, in_=sh), reads=[t_msel], writes=[self.consts_t])
                f = 0.5 if s != 1 else 1.0
                P.op("dve", lambda e, d=Gd, g=gate, f=f: e.tensor_scalar(out=d, in0=g, scalar1=f, scalar2=None, op0=ALU.mult),
                     reads=[t_msel], writes=[self.consts_t])

    def modulation(self):
        self.mod_setup()
        if self.stop_after in ("mod", "ffn00"):
            self.mod_tiles(0, range(144), bank=7)
            self.mod_finish(0, bank=7)
        else:
            self.mod_tiles(0, range(48), bank=7)
            self.mod_finish(0, bank=7, subs=(0,))
            self.mod0_rest = list(range(48, 144))
        if self.stop_after in ("mod", "ffn00"):
            self.mod_tiles(1, range(144), bank=1)
            self.mod_finish(1, bank=1)

    def mod(self, l, w, s, which):
        k = {"A": 0, "B": 1, "G": 2}[which]
        return self.modtab[:, l, w, (3 * s + k) * KC:(3 * s + k + 1) * KC]

    def norm_mod(self, x3, t_x, h3, t_h, T, l, w, s):
        P = self.P
        A32 = self.A32
        sq = [A32.take(512), A32.take(512)]
        t_sq = [P.tk(), P.tk()]
        rstd = A32.take(512)
        t_rstd = P.tk()
        ps, pst = self.psum[0], self.psum_t[0]
        for kc in range(KC):
            P.op("act", lambda e, o=sq[kc % 2][:, :T], i=x3[:, kc, :]: e.activation(out=o, in_=i, func=AF.Square),
                 reads=[t_x], writes=[t_sq[kc % 2]])
            P.mm(ps[:, :T], self.ones32[:], sq[kc % 2][:, :T], kc == 0, kc == KC - 1, reads=[t_sq[kc % 2], self.consts_t], writes=[pst], ms=True)
        P.op("act", lambda e: e.activation(out=rstd[:, :T], in_=ps[:, :T], func=AF.Sqrt, bias=self.epsc[:, 0:1], scale=1.0 / D),
             reads=[pst, self.consts_t], writes=[t_rstd])
        P.op("dve", lambda e: e.reciprocal(out=rstd[:, :T], in_=rstd[:, :T]), reads=[t_rstd], writes=[t_rstd])
        Am = self.mod(l, w, s, "A")
        Bm = self.mod(l, w, s, "B")
        for kc in range(KC):
            P.op("dve", lambda e, o=sq[kc % 2][:, :T], i=x3[:, kc, :], a=Am[:, kc:kc + 1]: e.scalar_tensor_tensor(
                out=o, in0=i, scalar=a, in1=rstd[:, :T], op0=ALU.mult, op1=ALU.mult),
                reads=[t_x, t_rstd, self.consts_t], writes=[t_sq[kc % 2]])
            P.op("act", lambda e, o=h3[:, kc, :], i=sq[kc % 2][:, :T], b=Bm[:, kc:kc + 1]: e.activation(
                out=o, in_=i, func=AF.Identity, bias=b, scale=1.0), reads=[t_sq[kc % 2], self.consts_t], writes=[t_h])

    def chunks(self, with_ctx=True, nlat=NCH_ALL):
        ch = [(i * 512, 512, 0) for i in range(nlat)]
        if with_ctx:
            ch.append((LAT, CTX, 1))
        return ch

    def _mod0_hook(self, j):
        if j is None:
            assert not self.mod0_rest
            self.mod_finish(0, bank=7, subs=(1, 2))
            return
        for _ in range(3):
            if self.mod0_rest:
                self.mod_tiles(0, [self.mod0_rest.pop(0)], bank=7)

    def ffn(self, l, s, src_fn, dst_fn, with_ctx=True, nlat=NCH_ALL, hook=None):
        P = self.P
        win = f"f_in_{l}{s}"
        wout = f"f_out_{l}{s}"
        sub = 0 if s == 0 else 2
        chs = self.chunks(with_ctx, nlat)
        A32, A16 = self.A32, self.A16
        A32.reset(); A16.reset()
        x32 = A32.take(KC * 512); t_x = P.tk()
        h16 = [A16.take(KC * 512) for _ in range(2)]; t_h = [P.tk() for _ in range(2)]
        act16 = A16.take(FC * 512); t_a = [P.tk() for _ in range(FC)]
        sq = [A32.take(512) for _ in range(4)]; t_sq = [P.tk() for _ in range(4)]
        rstd = A32.take(512); t_rstd = P.tk()
        sil = [A32.take(512) for _ in range(2)]; t_sil = [P.tk() for _ in range(2)]
        yo = [A32.take(512) for _ in range(2)]; t_yo = [P.tk() for _ in range(2)]
        xr = [A32.take(512) for _ in range(2)]; t_xr = [P.tk() for _ in range(2)]
        pss, psst = self.psum[0], self.psum_t[0]

        def views(c):
            t0, T, w = chs[c]
            x3 = x32[:, :KC * T].rearrange("p (k t) -> p k t", t=T)
            h3 = h16[c % 2][:, :KC * T].rearrange("p (k t) -> p k t", t=T)
            return t0, T, w, x3, h3

        def load_x(c):
            t0, T, w, x3, h3 = views(c)
            src, t_src = src_fn((t0, T, w))
            P.dma("act", x3, src.rearrange("(k p) t -> p k t", p=128), reads=[t_src] if t_src else [], writes=[t_x])

        def square(c, kc):
            t0, T, w, x3, h3 = views(c)
            P.op("act", lambda e, o=sq[kc % 4][:, :T], i=x3[:, kc, :]: e.activation(out=o, in_=i, func=AF.Square), reads=[t_x], writes=[t_sq[kc % 4]])

        def onesmm(c, kc):
            t0, T, w, x3, h3 = views(c)
            P.mm(pss[:, :T], self.ones32[:], sq[kc % 4][:, :T], kc == 0, kc == KC - 1, reads=[t_sq[kc % 4], self.consts_t], writes=[psst], ms=True)

        def normB(c):
            t0, T, w, x3, h3 = views(c)
            P.op("act", lambda e: e.activation(out=rstd[:, :T], in_=pss[:, :T], func=AF.Sqrt, bias=self.epsc[:, 0:1], scale=1.0 / D), reads=[psst, self.consts_t], writes=[t_rstd])
            P.op("dve", lambda e: e.reciprocal(out=rstd[:, :T], in_=rstd[:, :T]), reads=[t_rstd], writes=[t_rstd])
            Am = self.mod(l, w, sub, "A")
            Bm = self.mod(l, w, sub, "B")
            for kc in range(KC):
                P.op("dve", lambda e, o=sq[kc % 4][:, :T], i=x3[:, kc, :], a=Am[:, kc:kc + 1]: e.scalar_tensor_tensor(
                    out=o, in0=i, scalar=a, in1=rstd[:, :T], op0=ALU.mult, op1=ALU.mult), reads=[t_x, t_rstd, self.consts_t], writes=[t_sq[kc % 4]])
                P.op("act", lambda e, o=h3[:, kc, :], i=sq[kc % 4][:, :T], b=Bm[:, kc:kc + 1]: e.activation(
                    out=o, in_=i, func=AF.Identity, bias=b, scale=1.0), reads=[t_sq[kc % 4], self.consts_t], writes=[t_h[c % 2]])

        load_x(0)
        for kc in range(KC):
            square(0, kc)
            onesmm(0, kc)
        normB(0)
        for c in range(len(chs)):
            t0, T, w, x3, h3 = views(c)
            th = t_h[c % 2]
            a3 = act16[:, :FC * T].rearrange("p (k t) -> p k t", t=T)
            src, t_src = src_fn((t0, T, w))
            dst, t_dst = dst_fn((t0, T, w))
            src3 = src.rearrange("(k p) t -> p k t", p=128)
            dst3 = dst.rearrange("(k p) t -> p k t", p=128)
            if c + 1 < len(chs):
                load_x(c + 1)
            for j in range(FC):
                wg, tg = self.wtile(win, j)
                pg, pgt = self.psum[1 + j % 2], self.psum_t[1 + j % 2]
                for kc in range(KC):
                    P.mm(pg[:, :T], wg[:, kc, :], h3[:, kc, :], kc == 0, kc == KC - 1, reads=[tg, th], writes=[pgt])
                wu, tu = self.wtile(win, FC + j)
                pu, put = self.psum[3 + j % 2], self.psum_t[3 + j % 2]
                for kc in range(KC):
                    P.mm(pu[:, :T], wu[:, kc, :], h3[:, kc, :], kc == 0, kc == KC - 1, reads=[tu, th], writes=[put])
                P.op("act", lambda e, o=sil[j % 2][:, :T], i=pg[:, :T]: e.activation(out=o, in_=i, func=AF.Silu), reads=[pgt], writes=[t_sil[j % 2]])
                P.op("dve", lambda e, o=a3[:, j, :], a=sil[j % 2][:, :T], b=pu[:, :T]: e.tensor_tensor(out=o, in0=a, in1=b, op=ALU.mult),
                     reads=[t_sil[j % 2], put], writes=[t_a[j]])
                if hook is not None and c == 0:
                    hook(j)
            if hook is not None and c == 0:
                hook(None)
            Gm = self.mod(l, w, sub, "G")
            nxt = c + 1 < len(chs)
            for i in range(KC):
                P.dma("sp", xr[i % 2][:, :T], src3[:, i, :], reads=[], writes=[t_xr[i % 2]])
                wo, to = self.wtile(wout, i)
                py, pyt = self.psum[5 + i % 2], self.psum_t[5 + i % 2]
                for j in range(FC):
                    P.mm(py[:, :T], wo[:, j, :], a3[:, j, :], j == 0, j == FC - 1, reads=[to, t_a[j]], writes=[pyt])
                if nxt and 1 <= i <= 8:
                    onesmm(c + 1, 2 * (i - 1))
                    onesmm(c + 1, 2 * (i - 1) + 1)
                P.op("dve", lambda e, o=yo[i % 2][:, :T], y=py[:, :T], g=Gm[:, i:i + 1], x=xr[i % 2][:, :T]: e.scalar_tensor_tensor(
                    out=o, in0=y, scalar=g, in1=x, op0=ALU.mult, op1=ALU.add), reads=[pyt, t_xr[i % 2], self.consts_t], writes=[t_yo[i % 2]])
                P.dma("act", dst3[:, i, :], yo[i % 2][:, :T], reads=[t_yo[i % 2]], writes=[t_dst], partial=True)
                if nxt:
                    if i <= 7:
                        square(c + 1, 2 * i)
                        square(c + 1, 2 * i + 1)
                    if i == 9:
                        normB(c + 1)
        P.barrier()

    def mixer_consts(self):
        P = self.P
        A32 = self.A32
        A32.reset()
        c32, c16, ct = self.cst32, self.cst16, self.consts_t
        t1 = P.tk(); t2 = P.tk(); t3 = P.tk()
        P.dma("sp", c32[:, C_SP:C_SP + NSP], self.smallp[:, :], writes=[ct])
        P.dma("sp", c32[:, C_MASK:C_MASK + 1024], self.maskPN[:, :], writes=[ct], partial=True)
        pr = A32.take(256)
        P.dma("sp", pr, self.prot[:, :], writes=[t1])
        P.op("dve", lambda e: e.tensor_copy(out=c16[:, 0:256], in_=pr), reads=[t1], writes=[ct])
        for hf in range(2):
            gw = A32.take(2048)
            tg = P.tk()
            P.dma("sp", gw, self.gatew[:, hf * 2048:(hf + 1) * 2048], writes=[tg])
            P.op("dve", lambda e, gw=gw, hf=hf: e.tensor_copy(out=c16[:, 256 + hf * 2048:256 + (hf + 1) * 2048], in_=gw), reads=[tg], writes=[ct])
        sp8 = c32[:, C_SP8:C_SP8 + 16]
        ev = A32.take(16); l1 = A32.take(16); ec = A32.take(16); pl = A32.take(16); mk_ = A32.take(16)
        tq = P.tk()
        P.op("act", lambda e: e.activation(out=ev, in_=c32[:, SP_LM:SP_LM + 16], func=AF.Exp, scale=-1.0), reads=[ct], writes=[tq])
        P.op("act", lambda e: e.activation(out=l1, in_=ev, func=AF.Ln, bias=self.ones32[:, 0:1], scale=1.0), reads=[tq, ct], writes=[tq])
        P.op("dve", lambda e: e.tensor_scalar(out=ec, in0=ev, scalar1=0.05, scalar2=None, op0=ALU.min), reads=[tq], writes=[tq])
        P.op("dve", lambda e: e.tensor_scalar(out=pl, in0=ec, scalar1=0.2, scalar2=-0.25, op0=ALU.mult, op1=ALU.add), reads=[tq], writes=[tq])
        for cf in (1.0 / 3, -0.5, 1.0):
            P.op("dve", lambda e: e.tensor_tensor(out=pl, in0=pl, in1=ec, op=ALU.mult), reads=[tq], writes=[tq])
            P.op("dve", lambda e, cf=cf: e.tensor_scalar(out=pl, in0=pl, scalar1=float(cf), scalar2=None, op0=ALU.add), reads=[tq], writes=[tq])
        P.op("dve", lambda e: e.tensor_tensor(out=pl, in0=pl, in1=ec, op=ALU.mult), reads=[tq], writes=[tq])
        P.op("dve", lambda e: e.tensor_scalar(out=mk_, in0=ev, scalar1=0.05, scalar2=None, op0=ALU.is_gt), reads=[tq], writes=[tq])
        P.op("dve", lambda e: e.tensor_tensor(out=l1, in0=l1, in1=pl, op=ALU.subtract), reads=[tq], writes=[tq])
        P.op("dve", lambda e: e.tensor_tensor(out=l1, in0=l1, in1=mk_, op=ALU.mult), reads=[tq], writes=[tq])
        P.op("dve", lambda e: e.tensor_tensor(out=l1, in0=l1, in1=pl, op=ALU.add), reads=[tq], writes=[tq])
        P.op("dve", lambda e: e.tensor_scalar(out=sp8, in0=l1, scalar1=-8.0, scalar2=None, op0=ALU.mult), reads=[tq], writes=[ct])
        P.op("dve", lambda e: e.tensor_scalar(out=c32[:, C_GQS:C_GQS + 1], in0=c32[:, SP_QG:SP_QG + 1], scalar1=float(128 ** -0.5), scalar2=None, op0=ALU.mult), reads=[ct], writes=[ct])
        P.op("dve", lambda e: e.tensor_scalar(out=c32[:, C_NQS:C_NQS + 1], in0=c32[:, SP_NQG:SP_NQG + 1], scalar1=float(128 ** -0.5), scalar2=None, op0=ALU.mult), reads=[ct], writes=[ct])
        P.op("act", lambda e: e.activation(out=c32[:, C_ESINK:C_ESINK + 8], in_=c32[:, SP_SK:SP_SK + 8], func=AF.Exp), reads=[ct], writes=[ct])
        if "dbg" in self.debug:
            self.C32o = self.ext_out("C32o", [128, 1200])
            P.dma("act", self.C32o[:, :], c32[:, :], reads=[ct], writes=[P.tk()])
        P.barrier()

    def qknorm(self, ps, pst, T, gcol, bufs, bank=5):
        P = self.P
        sqq, t_sqq, rs, t_rs, qn, t_qn = bufs
        ps5, ps5t = self.psum[bank], self.psum_t[bank]
        P.op("act", lambda e: e.activation(out=sqq[:, :T], in_=ps[:, :T], func=AF.Square), reads=[pst], writes=[t_sqq])
        P.mm(ps5[:, :T], self.ones32[:], sqq[:, :T], True, True, reads=[t_sqq, self.consts_t], writes=[ps5t])
        P.op("act", lambda e: e.activation(out=rs[:, :T], in_=ps5[:, :T], func=AF.Sqrt, bias=self.epsc[:, 0:1], scale=1.0 / 128), reads=[ps5t, self.consts_t], writes=[t_rs])
        P.op("dve", lambda e: e.reciprocal(out=rs[:, :T], in_=rs[:, :T]), reads=[t_rs], writes=[t_rs])
        P.op("dve", lambda e: e.scalar_tensor_tensor(out=qn[:, :T], in0=ps[:, :T], scalar=gcol, in1=rs[:, :T], op0=ALU.mult, op1=ALU.mult),
             reads=[pst, t_rs, self.consts_t], writes=[t_qn])


    def proj_pipeline(self, wname, tiles, h3, t_h, T, cs=None, t_cs=None):
        P = self.P
        A32, A16 = self.A32, self.A16
        c16, ct = self.cst16, self.consts_t
        o32 = [A32.take(512) for _ in range(2)]; t_o32 = [P.tk() for _ in range(2)]
        sqq = [A32.take(512) for _ in range(2)]; t_sqq = [P.tk() for _ in range(2)]
        rs = A32.take(512); t_rs = P.tk()
        qn = [A32.take(512) for _ in range(3)]; t_qn = [P.tk() for _ in range(3)]
        tt1 = A32.take(512); t_tt1 = P.tk()
        tt2 = A32.take(512); t_tt2 = P.tk()
        o16 = [A16.take(512) for _ in range(2)]; t_o16 = [P.tk() for _ in range(2)]
        q16 = [A16.take(512) for _ in range(2)]; t_q16 = [P.tk() for _ in range(2)]
        nbank = [5, 0]

        def store(tl, o, to):
            P.dma("pool", tl["dst"], o[:, :T], reads=[to], writes=[tl["td"]], partial=True, pn="pst")

        def s0(i, tl):
            wt, tw = self.wtile(wname, tl["n"])
            ps, pst = self.psum[1 + i % 4], self.psum_t[1 + i % 4]
            for kc in range(KC):
                P.mm(ps[:, :T], wt[:, kc, :], h3[:, kc, :], kc == 0, kc == KC - 1, reads=[tw, t_h], writes=[pst])
            if tl["kind"] in ("copy", "gelu"):
                o, to = o32[i % 2], t_o32[i % 2]
                fn = AF.Copy if tl["kind"] == "copy" else AF.Gelu_apprx_tanh
                P.op("act", lambda e, o=o, ps=ps, fn=fn: e.activation(out=o[:, :T], in_=ps[:, :T], func=fn), reads=[pst], writes=[to])
                store(tl, o, to)
            else:
                P.op("act", lambda e, o=sqq[i % 2], ps=ps: e.activation(out=o[:, :T], in_=ps[:, :T], func=AF.Square), reads=[pst], writes=[t_sqq[i % 2]])

        def s1(i, tl):
            if tl["kind"] != "qk":
                return
            ps, pst = self.psum[1 + i % 4], self.psum_t[1 + i % 4]
            pn_, pnt = self.psum[nbank[i % 2]], self.psum_t[nbank[i % 2]]
            P.mm(pn_[:, :T], self.ones32[:], sqq[i % 2][:, :T], True, True, reads=[t_sqq[i % 2], ct], writes=[pnt])
            P.op("act", lambda e, pn_=pn_: e.activation(out=rs[:, :T], in_=pn_[:, :T], func=AF.Sqrt, bias=self.epsc[:, 0:1], scale=1.0 / 128), reads=[pnt, ct], writes=[t_rs])
            P.op("dve", lambda e: e.reciprocal(out=rs[:, :T], in_=rs[:, :T]), reads=[t_rs], writes=[t_rs])
            Q, tQ = qn[i % 3], t_qn[i % 3]
            P.op("dve", lambda e, Q=Q, ps=ps, g=tl["gcol"]: e.scalar_tensor_tensor(out=Q[:, :T], in0=ps[:, :T], scalar=g, in1=rs[:, :T], op0=ALU.mult, op1=ALU.mult),
                 reads=[pst, t_rs, ct], writes=[tQ])
            if tl["rope"]:
                P.op("act", lambda e, Q=Q, o=q16[i % 2]: e.activation(out=o[:, :T], in_=Q[:, :T], func=AF.Copy), reads=[tQ], writes=[t_q16[i % 2]])
            else:
                o, to = o16[i % 2], t_o16[i % 2]
                P.op("act", lambda e, Q=Q, o=o: e.activation(out=o[:, :T], in_=Q[:, :T], func=AF.Copy), reads=[tQ], writes=[to])
                store(tl, o, to)

        def s2(i, tl):
            if tl["kind"] != "qk" or not tl["rope"]:
                return
            Q, tQ = qn[i % 3], t_qn[i % 3]
            ps6, ps6t = self.psum[6], self.psum_t[6]
            P.mm(ps6[:, :T], c16[:, 0:128], q16[i % 2][:, :T], True, True, reads=[t_q16[i % 2], ct], writes=[ps6t])
            P.op("dve", lambda e, Q=Q: e.tensor_tensor(out=tt1[:, :T], in0=Q[:, :T], in1=cs[:, 0:T], op=ALU.mult), reads=[tQ, t_cs], writes=[t_tt1])
            P.op("dve", lambda e: e.tensor_tensor(out=tt2[:, :T], in0=ps6[:, :T], in1=cs[:, 512:512 + T], op=ALU.mult), reads=[ps6t, t_cs], writes=[t_tt2])
            o, to = o16[i % 2], t_o16[i % 2]
            P.op("dve", lambda e, o=o: e.tensor_tensor(out=o[:, :T], in0=tt1[:, :T], in1=tt2[:, :T], op=ALU.add), reads=[t_tt1, t_tt2], writes=[to])
            store(tl, o, to)

        nt = len(tiles)
        for step in range(nt + 2):
            if step < nt:
                s0(step, tiles[step])
            if 0 <= step - 1 < nt:
                s1(step - 1, tiles[step - 1])
            if 0 <= step - 2 < nt:
                s2(step - 2, tiles[step - 2])

    def mixer_ab(self):
        P = self.P
        A32, A16 = self.A32, self.A16
        c32, c16, ct = self.cst32, self.cst16, self.consts_t
        l = 0
        ch = self.chunks()
        t_XL = P.tk(phase=False); t_GG = P.tk(phase=False); t_QT = P.tk(phase=False); t_KT = P.tk(phase=False)
        t_VV = P.tk(phase=False); t_HP = P.tk(phase=False); t_CAT = P.tk(phase=False)
        for ci, (t0, T, w) in enumerate(ch):
            A32.reset(); A16.reset()
            P.tk_scope("m0a")
            x32 = A32.take(KC * 512); x3 = x32[:, :KC * T].rearrange("p (k t) -> p k t", t=T); t_x = P.tk()
            h16 = A16.take(KC * 512); h3 = h16[:, :KC * T].rearrange("p (k t) -> p k t", t=T); t_h = P.tk()
            P.dma("sp", x3, self.XT[:, t0:t0 + T].rearrange("(k p) t -> p k t", p=128), reads=[], writes=[t_x])
            self.norm_mod(x3, t_x, h3, t_h, T, l, w, 1)
            cs = A32.take(1024); t_cs = P.tk()
            if not w:
                P.dma("sp", cs[:, 0:T], self.ropeT[:, t0:t0 + T], writes=[t_cs])
                P.dma("sp", cs[:, 512:512 + T], self.ropeT[:, LAT + t0:LAT + t0 + T], writes=[t_cs], partial=True)
            v16 = [A16.take(256) for _ in range(2)]; t_v16 = [P.tk() for _ in range(2)]
            ext = w or (t0 // 512) < NCH_EXT
            tiles = []
            for n in range(26 if ext else 8):
                if n < 8:
                    tiles.append(dict(n=n, kind="copy", dst=self.XL[n * 128:(n + 1) * 128, t0:t0 + T], td=t_XL))
                elif n < 16:
                    tiles.append(dict(n=n, kind="gelu", dst=self.GG[(n - 8) * 128:(n - 7) * 128, t0:t0 + T], td=t_GG))
                elif n < 24:
                    tiles.append(dict(n=n, kind="qk", gcol=c32[:, C_GQS:C_GQS + 1], rope=not w, dst=self.QT[(n - 16) * 128:(n - 15) * 128, t0:t0 + T], td=t_QT))
                else:
                    tiles.append(dict(n=n, kind="qk", gcol=c32[:, SP_KG:SP_KG + 1], rope=not w, dst=self.KT[(n - 24) * 128:(n - 23) * 128, t0:t0 + T], td=t_KT))
            self.proj_pipeline("ab_in", tiles, h3, t_h, T, cs, t_cs)
            if not ext:
                continue
            wv0, tv0 = self.wtile("ab_in", 26)
            wv1, tv1 = self.wtile("ab_in", 27)
            ps7, ps7t = self.psum[7], self.psum_t[7]
            for tb in range(T // 128):
                for hv, (wv, tv) in enumerate(((wv0, tv0), (wv1, tv1))):
                    for kc in range(KC):
                        P.mm(ps7[:, hv * 128:(hv + 1) * 128], h3[:, kc, tb * 128:(tb + 1) * 128], wv[:, kc, :], kc == 0, kc == KC - 1,
                             reads=[tv, t_h], writes=[ps7t])
                vo, tvo = v16[tb % 2], t_v16[tb % 2]
                P.op("act", lambda e, vo=vo: e.activation(out=vo, in_=ps7[:, 0:256], func=AF.Copy), reads=[ps7t], writes=[tvo])
                P.dma("pool", self.VV[t0 + tb * 128:t0 + (tb + 1) * 128, :], vo, reads=[tvo], writes=[t_VV], partial=True, pn="pst")
        P.tk_scope(None)
        P.barrier()
        for t in (t_XL, t_GG, t_QT, t_KT, t_VV):
            t.w = []; t.r = {}
        lat_ch = [c for c in ch if not c[2]]
        ctx_ch = [c for c in ch if c[2]]
        gw3 = c16[:, 256:256 + 4096].rearrange("p (k m n) -> p k m n", k=2, m=16)
        A32.reset(); A16.reset()
        def bufs32(cnt, n_=512):
            return [A32.take(n_) for _ in range(cnt)], [P.tk() for _ in range(cnt)]
        Xb, tXb = bufs32(2, 516)
        Ub, tUb = bufs32(5)
        Rb, tRb = bufs32(3)
        Ib, tIb = bufs32(3)
        A2b, tA2b = bufs32(2)
        Hb, tHb = bufs32(2)
        HPb, tHPb = bufs32(2)
        GGb, tGGb = bufs32(2)
        U16b = [A16.take(512) for _ in range(2)]; tU16b = [P.tk() for _ in range(2)]
        OCb = [A16.take(512) for _ in range(2)]; tOCb = [P.tk() for _ in range(2)]
        carry = A32.take(8); t_carry = [P.tk() for _ in range(8)]
        mod_gc = [0]

        def st0(j, d, idx, t0, T, w, n):
            seg0, seg1 = (LAT, LAT + CTX) if w else (0, LAT)
            lo, hi = max(seg0, t0 - 2), min(seg1, t0 + T + 2)
            rows = slice(n * 128, (n + 1) * 128)
            X, tX = Xb[j % 2], tXb[j % 2]
            if lo != t0 - 2 or hi != t0 + T + 2:
                P.op("dve", lambda e, X=X: e.memset(X[:, 0:516], 0.0), writes=[tX])
            P.dma("sp", X[:, lo - (t0 - 2):hi - (t0 - 2)], self.XL[rows, lo:hi], reads=[t_XL], writes=[tX])
            U, tU = Ub[j % 5], tUb[j % 5]
            P.op("dve", lambda e, X=X, U=U, T=T, n=n: e.tensor_scalar(out=U[:, :T], in0=X[:, 0:T], scalar1=c32[:, SP_CW + n * 5:SP_CW + n * 5 + 1],
                                                         scalar2=c32[:, SP_CB + n:SP_CB + n + 1], op0=ALU.mult, op1=ALU.add), reads=[tX, ct], writes=[tU])
            for kk in range(1, 5):
                P.op("dve", lambda e, X=X, U=U, kk=kk, T=T, n=n: e.scalar_tensor_tensor(out=U[:, :T], in0=X[:, kk:kk + T], scalar=c32[:, SP_CW + n * 5 + kk:SP_CW + n * 5 + kk + 1],
                                                                           in1=U[:, :T], op0=ALU.mult, op1=ALU.add), reads=[tX, ct, tU], writes=[tU])

        def st1(j, d, idx, t0, T, w, n):
            U, tU, U16, tU16 = Ub[j % 5], tUb[j % 5], U16b[j % 2], tU16b[j % 2]
            P.op("act", lambda e, U=U, U16=U16, T=T: e.activation(out=U16[:, :T], in_=U[:, :T], func=AF.Copy), reads=[tU], writes=[tU16])
            k = j % 2
            psA, psAt = self.psum[1 + 2 * k], self.psum_t[1 + 2 * k]
            psX, psXt = self.psum[2 + 2 * k], self.psum_t[2 + 2 * k]
            P.mm(psA[:, :T], gw3[:, 0, d * 8 + n, :], U16[:, :T], True, True, reads=[tU16, ct], writes=[psAt])
            P.mm(psX[:, :T], gw3[:, 1, d * 8 + n, :], U16[:, :T], True, True, reads=[tU16, ct], writes=[psXt])

        def st2(j, d, idx, t0, T, w, n):
            dn = d * 8 + n
            k = j % 2
            psA, psAt = self.psum[1 + 2 * k], self.psum_t[1 + 2 * k]
            psX, psXt = self.psum[2 + 2 * k], self.psum_t[2 + 2 * k]
            R, tR, I, tI = Rb[j % 3], tRb[j % 3], Ib[j % 3], tIb[j % 3]
            P.op("act", lambda e, psA=psA, R=R, T=T, dn=dn: e.activation(out=R[:, :T], in_=psA[:, :T], func=AF.Sigmoid, bias=c32[:, SP_BA + dn:SP_BA + dn + 1], scale=1.0), reads=[psAt, ct], writes=[tR])
            P.op("act", lambda e, psX=psX, I=I, T=T, dn=dn: e.activation(out=I[:, :T], in_=psX[:, :T], func=AF.Sigmoid, bias=c32[:, SP_BX + dn:SP_BX + dn + 1], scale=1.0), reads=[psXt, ct], writes=[tI])
            P.op("act", lambda e, R=R, T=T, dn=dn: e.activation(out=R[:, :T], in_=R[:, :T], func=AF.Exp, scale=c32[:, C_SP8 + dn:C_SP8 + dn + 1]), reads=[tR, ct], writes=[tR])

        def st3(j, d, idx, t0, T, w, n):
            R, tR, A2, tA2 = Rb[j % 3], tRb[j % 3], A2b[j % 2], tA2b[j % 2]
            P.op("dve", lambda e, R=R, A2=A2, T=T: e.tensor_tensor(out=A2[:, :T], in0=R[:, :T], in1=R[:, :T], op=ALU.mult), reads=[tR], writes=[tA2])
            P.op("act", lambda e, A2=A2, T=T: e.activation(out=A2[:, :T], in_=A2[:, :T], func=AF.Sqrt, bias=self.ones32[:, 0:1], scale=-1.0), reads=[tA2, ct], writes=[tA2])

        def st4(j, d, idx, t0, T, w, n):
            rows = slice(n * 128, (n + 1) * 128)
            need_out = w or (t0 // 512) < NCH_EXT
            R, tR, I, tI, A2, tA2 = Rb[j % 3], tRb[j % 3], Ib[j % 3], tIb[j % 3], A2b[j % 2], tA2b[j % 2]
            U, tU, H, tH = Ub[j % 5], tUb[j % 5], Hb[j % 2], tHb[j % 2]
            P.op("dve", lambda e, I=I, A2=A2, T=T: e.tensor_tensor(out=I[:, :T], in0=A2[:, :T], in1=I[:, :T], op=ALU.mult), reads=[tA2, tI], writes=[tI])
            P.op("dve", lambda e, I=I, U=U, T=T: e.tensor_tensor(out=I[:, :T], in0=I[:, :T], in1=U[:, :T], op=ALU.mult), reads=[tI, tU], writes=[tI])
            if idx == 0:
                init, rd = 0.0, []
            else:
                init, rd = carry[:, n:n + 1], [t_carry[n]]
            if d == 0:
                P.op("dve", lambda e, H=H, R=R, I=I, init=init, T=T: e.tensor_tensor_scan(out=H[:, :T], data0=R[:, :T], data1=I[:, :T], initial=init, op0=ALU.mult, op1=ALU.add),
                     reads=[tR, tI] + rd, writes=[tH])
                P.op("dve", lambda e, H=H, T=T, n=n: e.tensor_copy(out=carry[:, n:n + 1], in_=H[:, T - 1:T]), reads=[tH], writes=[t_carry[n]])
                if need_out:
                    P.dma("pool", self.HP[rows, t0:t0 + T], H[:, :T], reads=[tH], writes=[t_HP], partial=True, pn="pst")
            else:
                P.op("dve", lambda e, H=H, R=R, I=I, init=init, T=T: e.tensor_tensor_scan(out=H[:, T - 1::-1], data0=R[:, T - 1::-1], data1=I[:, T - 1::-1], initial=init, op0=ALU.mult, op1=ALU.add),
                     reads=[tR, tI] + rd, writes=[tH])
                P.op("dve", lambda e, H=H, n=n: e.tensor_copy(out=carry[:, n:n + 1], in_=H[:, 0:1]), reads=[tH], writes=[t_carry[n]])
                if not need_out:
                    return
                HPc, tHP, GGc, tGG = HPb[j % 2], tHPb[j % 2], GGb[j % 2], tGGb[j % 2]
                P.dma("sp", HPc[:, :T], self.HP[rows, t0:t0 + T], reads=[t_HP], writes=[tHP])
                P.dma("sp", GGc[:, :T], self.GG[rows, t0:t0 + T], reads=[t_GG], writes=[tGG])
                P.op("dve", lambda e, H=H, HPc=HPc, T=T: e.tensor_tensor(out=HPc[:, :T], in0=H[:, :T], in1=HPc[:, :T], op=ALU.add), reads=[tH, tHP], writes=[tHP])
                OC, tOC = OCb[j % 2], tOCb[j % 2]
                P.op("dve", lambda e, OC=OC, GGc=GGc, HPc=HPc, T=T: e.tensor_tensor(out=OC[:, :T], in0=HPc[:, :T], in1=GGc[:, :T], op=ALU.mult), reads=[tHP, tGG], writes=[tOC])
                P.dma("pool", self.CAT[rows, t0:t0 + T], OC[:, :T], reads=[tOC], writes=[t_CAT], partial=True, pn="pst")

        stages = [st0, st1, st2, st3, st4]
        for d in range(2):
            order = ctx_ch + (lat_ch if d == 0 else lat_ch[::-1])
            jobs = [(d, idx, t0, T, w, n) for idx, (t0, T, w) in enumerate(order) for n in range(8)]
            for step in range(len(jobs) + len(stages) - 1):
                for si, st in enumerate(stages):
                    j = step - si
                    if 0 <= j < len(jobs):
                        st(j, *jobs[j])
                if mod_gc[0] < 144:
                    self.mod_tiles(1, [mod_gc[0]], bank=0)
                    mod_gc[0] += 1
            self._m0b_barrier(t_HP)
        assert mod_gc[0] == 144, mod_gc[0]
        self.mod_finish(1, bank=0)
        self._m0b_barrier(t_HP)
        NB = NT // 128
        NLB = LAT // 128
        for g in range(2):
            A32.reset(); A16.reset()
            K16 = A16.take(NT); t_K = P.tk()
            V16 = A16.take(NT); t_V = P.tk()
            Q16 = A16.take(4 * NT); t_Q = P.tk()
            V3 = V16.rearrange("p (b c) -> p b c", c=128)
            Q3 = Q16.rearrange("p (j t) -> p j t", j=4)
            P.dma("sp", K16, self.KT[g * 128:(g + 1) * 128, :], reads=[t_KT], writes=[t_K])
            P.dma("sp", V3, self.VV.rearrange("(b p) c -> p b c", p=128)[:, :, g * 128:(g + 1) * 128], reads=[t_VV], writes=[t_V])
            P.dma("sp", Q3, self.QT[4 * g * 128:(4 * g + 4) * 128, :].rearrange("(j p) t -> p j t", p=128), reads=[t_QT], writes=[t_Q])
            esr = A32.take(512); t_esr = P.tk()
            for j in range(4):
                P.op("dve", lambda e, j=j, g=g: e.tensor_scalar(out=esr[:, j * 128:(j + 1) * 128], in0=self.ones32[:, 0:128], scalar1=c32[:, C_ESINK + 4 * g + j:C_ESINK + 4 * g + j + 1],
                                                        scalar2=None, op0=ALU.mult), reads=[ct, t_esr], writes=[t_esr])
            E = [A16.take(512) for _ in range(5)]; t_E = [P.tk() for _ in range(5)]
            sm = [A32.take(512) for _ in range(2)]; t_sm = [P.tk() for _ in range(2)]
            den = A32.take(512); t_den = P.tk()
            ob = [A16.take(512) for _ in range(2)]; t_ob = [P.tk() for _ in range(2)]
            qbl = list(range(NCH_EXT * 4 if HALF else NLB))
            if HALF:
                qbl = qbl[:18]
            for qb in qbl + [NLB, NLB + 1]:
                isctx = qb >= NLB
                kbs = []
                if not isctx:
                    if qb > 0:
                        kbs.append((qb - 1, "P"))
                    kbs.append((qb, "C"))
                    if qb < NLB - 1:
                        kbs.append((qb + 1, "N"))
                kbs += [(NLB, "C"), (NLB + 1, "C")]
                Q4 = Q3[:, :, qb * 128:(qb + 1) * 128]
                nm = 0
                for si, (kb, kind) in enumerate(kbs):
                    pS, pSt = self.psum[si], self.psum_t[si]
                    P.mm(pS[:, :].rearrange("p (j t) -> p j t", j=4), K16[:, kb * 128:(kb + 1) * 128], Q4, True, True, reads=[t_K, t_Q], writes=[pSt])
                    if kind == "C":
                        P.op("act", lambda e, pS=pS, Ei=E[si]: e.activation(out=Ei, in_=pS[:, :], func=AF.Exp), reads=[pSt], writes=[t_E[si]])
                    else:
                        mk = c32[:, C_MASK:C_MASK + 512] if kind == "P" else c32[:, C_MASK + 512:C_MASK + 1024]
                        S_, tS = sm[nm % 2], t_sm[nm % 2]; nm += 1
                        P.op("dve", lambda e, pS=pS, S_=S_, mk=mk: e.tensor_tensor(out=S_, in0=pS[:, :], in1=mk, op=ALU.add), reads=[pSt, ct], writes=[tS])
                        P.op("act", lambda e, S_=S_, Ei=E[si]: e.activation(out=Ei, in_=S_, func=AF.Exp), reads=[tS], writes=[t_E[si]])
                pO, pOt = self.psum[5], self.psum_t[5]
                pD, pDt = self.psum[6], self.psum_t[6]
                for si, (kb, kind) in enumerate(kbs):
                    P.mm(pO[:, :], V3[:, kb, :], E[si], si == 0, si == len(kbs) - 1, reads=[t_V, t_E[si]], writes=[pOt])
                for si, (kb, kind) in enumerate(kbs):
                    P.mm(pD[:, :], self.ones16[:], E[si], si == 0, si == len(kbs) - 1, reads=[ct, t_E[si]], writes=[pDt])
                P.op("dve", lambda e: e.tensor_tensor(out=den, in0=pD[:, :], in1=esr, op=ALU.add), reads=[pDt, t_esr], writes=[t_den])
                P.op("dve", lambda e: e.reciprocal(out=den, in_=den), reads=[t_den], writes=[t_den])
                O, tO = ob[qb % 2], t_ob[qb % 2]
                P.op("dve", lambda e, O=O: e.tensor_tensor(out=O, in0=pO[:, :], in1=den, op=ALU.mult), reads=[pOt, t_den], writes=[tO])
                r0 = (8 + 4 * g) * 128
                P.dma("pool", self.CAT[r0:r0 + 512, qb * 128:(qb + 1) * 128].rearrange("(j p) t -> p j t", p=128), O.rearrange("p (j t) -> p j t", j=4),
                      reads=[tO], writes=[t_CAT], partial=True, pn="pst")
            P.barrier()
        t_CAT.w = []; t_CAT.r = {}
        self.out_proj("ab_out", l, with_ctx=True, nlat=NCH_EXT)

    def _m0b_barrier(self, t_HP):
        P = self.P
        keep = [t for t in P.phase_tiles]
        P.barrier()
        for t in keep:
            P.phase_tiles.append(t)
        t_HP.w = []; t_HP.r = {}

    def out_proj(self, wname, l, with_ctx, nlat=NCH_ALL):
        P = self.P
        A32, A16 = self.A32, self.A16
        ch = self.chunks(with_ctx, nlat)
        for ci, (t0, T, w) in enumerate(ch):
            A32.reset(); A16.reset()
            x32 = A32.take(KC * 512); x3 = x32[:, :KC * T].rearrange("p (k t) -> p k t", t=T); t_x = P.tk()
            c16 = A16.take(KC * 512); c3 = c16[:, :KC * T].rearrange("p (k t) -> p k t", t=T); t_c = P.tk()
            yo = [A32.take(512), A32.take(512)]; t_yo = [P.tk(), P.tk()]
            P.dma("sp", x3, self.XT[:, t0:t0 + T].rearrange("(k p) t -> p k t", p=128), reads=[self.xt_tk(ci)], writes=[t_x])
            P.dma("sp", c3, self.CAT[:, t0:t0 + T].rearrange("(k p) t -> p k t", p=128), writes=[t_c])
            Gm = self.mod(l, w, 1, "G")
            dst3 = self.XT[:, t0:t0 + T].rearrange("(k p) t -> p k t", p=128)
            for i in range(KC):
                wo, to = self.wtile(wname, i)
                py, pyt = self.psum[5 + i % 2], self.psum_t[5 + i % 2]
                for j in range(KC):
                    P.mm(py[:, :T], wo[:, j, :], c3[:, j, :], j == 0, j == KC - 1, reads=[to, t_c], writes=[pyt])
                P.op("dve", lambda e, o=yo[i % 2][:, :T], y=py[:, :T], g=Gm[:, i:i + 1], x=x3[:, i, :]: e.scalar_tensor_tensor(
                    out=o, in0=y, scalar=g, in1=x, op0=ALU.mult, op1=ALU.add), reads=[pyt, t_x, self.consts_t], writes=[t_yo[i % 2]])
                P.dma("act", dst3[:, i, :], yo[i % 2][:, :T], reads=[t_yo[i % 2]], writes=[self.xt_tk(ci)], partial=True)
            P.barrier()


    def mixer_na(self):
        P = self.P
        A32, A16 = self.A32, self.A16
        c32, c16, ct = self.cst32, self.cst16, self.consts_t
        l = 1
        ch = self.chunks(True, NCH_EXT)
        t_Q = P.tk(phase=False); t_K = P.tk(phase=False); t_V = P.tk(phase=False); t_CAT = P.tk(phase=False)
        QT1, KT1, VV1 = self.QT1, self.KT1, self.VV1
        for ci, (t0, T, w) in enumerate(ch):
            A32.reset(); A16.reset()
            P.tk_scope("n1a")
            x32 = A32.take(KC * 512); x3 = x32[:, :KC * T].rearrange("p (k t) -> p k t", t=T); t_x = P.tk()
            h16 = A16.take(KC * 512); h3 = h16[:, :KC * T].rearrange("p (k t) -> p k t", t=T); t_h = P.tk()
            P.dma("sp", x3, self.XT[:, t0:t0 + T].rearrange("(k p) t -> p k t", p=128), reads=[], writes=[t_x])
            self.norm_mod(x3, t_x, h3, t_h, T, l, w, 1)
            v16 = [A16.take(512) for _ in range(2)]; t_v16 = [P.tk() for _ in range(2)]
            own = (not w) and (t0 // 512) < NCH_OWN
            tiles = []
            for n in range(0 if own else 16, 32):
                isq = n < 16
                gcol = c32[:, C_NQS:C_NQS + 1] if isq else c32[:, SP_NKG:SP_NKG + 1]
                dst, td = (QT1, t_Q) if isq else (KT1, t_K)
                r0 = (n % 16) * 128
                tiles.append(dict(n=n, kind="qk", gcol=gcol, rope=False, dst=dst[r0:r0 + 128, t0:t0 + T], td=td))
            self.proj_pipeline("na_in", tiles, h3, t_h, T)
            ps7, ps7t = self.psum[7], self.psum_t[7]
            ps6, ps6t = self.psum[6], self.psum_t[6]
            NTB = T // 128
            for n in range(16):
                wv, tv = self.wtile("na_in", 32 + n)
                pv, pvt = (ps7, ps7t) if n % 2 else (ps6, ps6t)
                for tb in range(NTB):
                    for kc in range(KC):
                        P.mm(pv[:, tb * 128:(tb + 1) * 128], h3[:, kc, tb * 128:(tb + 1) * 128], wv[:, kc, :], kc == 0, kc == KC - 1,
                             reads=[tv, t_h], writes=[pvt])
                vo, tvo = v16[n % 2], t_v16[n % 2]
                P.op("act", lambda e, vo=vo, pv=pv, T=T: e.activation(out=vo[:, :T], in_=pv[:, :T], func=AF.Copy), reads=[pvt], writes=[tvo])
                P.dma("pool", VV1[t0:t0 + T, n * 128:(n + 1) * 128].rearrange("(tb p) c -> p tb c", p=128),
                      vo[:, :T].rearrange("p (tb c) -> p tb c", c=128), reads=[tvo], writes=[t_V], partial=True, pn="pst")
        P.tk_scope(None)
        P.barrier()
        for t in (t_Q, t_K, t_V):
            t.w = []; t.r = {}
        NLB = LAT // 128
        NQB = OWN // 128
        A32.reset(); A16.reset()
        Kb = [A16.take(NT) for _ in range(2)]; tKb = [P.tk() for _ in range(2)]
        Qb_ = [A16.take(OWN) for _ in range(2)]; tQb = [P.tk() for _ in range(2)]
        Vb = [A16.take(NT) for _ in range(2)]; tVb = [P.tk() for _ in range(2)]
        OHb = [A16.take(OWN) for _ in range(2)]; tOHb = [[P.tk() for _ in range(NQB)] for _ in range(2)]
        TTb = [A32.take(15 * 64) for _ in range(2)]; tTTb = [P.tk() for _ in range(2)]
        TMb = [A16.take(9 * 128) for _ in range(2)]; tTMb = [[P.tk() for _ in range(9)] for _ in range(2)]
        MK = A32.take(3 * 128); t_MK = P.tk()
        E = [A16.take(8 * 128) for _ in range(2)]; t_E = [[P.tk(), P.tk()] for _ in range(2)]
        den = [A32.take(128) for _ in range(2)]; t_den = [P.tk() for _ in range(2)]
        ident = c16[:, 128:256]
        P.dma("sp", MK, self.namask[:, :], writes=[t_MK])

        def prologue(h):
            hp_ = h % 2
            P.dma("sp", Kb[hp_], KT1[h * 128:(h + 1) * 128, :], reads=[t_K], writes=[tKb[hp_]])
            P.dma("sp", Qb_[hp_], QT1[h * 128:(h + 1) * 128, 0:OWN], reads=[t_Q], writes=[tQb[hp_]])
            P.dma("sp", Vb[hp_].rearrange("p (b c) -> p b c", c=128), VV1.rearrange("(b p) c -> p b c", p=128)[:, :, h * 128:(h + 1) * 128], reads=[t_V], writes=[tVb[hp_]])
            P.dma("sp", TTb[hp_], self.natt[:, h * 960:(h + 1) * 960], writes=[tTTb[hp_]])
            for slot in range(9):
                o = slot - 3 if slot < 7 else (-2 if slot == 7 else 2)
                mi = 0 if slot < 7 else (1 if slot == 7 else 2)
                i0 = 7 - 2 * o
                P.op("dve", lambda e, slot=slot, i0=i0, mi=mi, TM=TMb[hp_], TT=TTb[hp_]: e.tensor_tensor(
                    out=TM[:, slot * 128:(slot + 1) * 128], in0=TT[:, i0 * 64:(i0 + 2) * 64], in1=MK[:, mi * 128:(mi + 1) * 128], op=ALU.add),
                    reads=[tTTb[hp_], t_MK], writes=[tTMb[hp_][slot]])

        def kbs_of(qb):
            if qb <= 1:
                return [0, 1, 2, 3]
            if qb >= NLB - 2:
                return [NLB - 4, NLB - 3, NLB - 2, NLB - 1]
            return [qb - 2, qb - 1, qb, qb + 1, qb + 2]

        def banks(t):
            pb = 4 * (t % 2)
            return [(self.psum[pb + i], self.psum_t[pb + i]) for i in range(4)]

        def stA(t, h, qb):
            hp_ = h % 2
            K16, Q16, TM, tms = Kb[hp_], Qb_[hp_], TMb[hp_], tTMb[hp_]
            (bA, bAt), (bB, bBt), _, _ = banks(t)
            edge = qb in (0, 1, NLB - 2, NLB - 1)
            kbs = kbs_of(qb)
            nb = len(kbs)
            Qb = Q16[:, qb * 128:(qb + 1) * 128]
            E_, tE = E[t % 2], t_E[t % 2]
            for si, kb in enumerate(kbs):
                bank, bt, col = (bA, bAt, si * 128) if si < 4 else (bB, bBt, 0)
                o = kb - qb
                slot = o + 3
                if not edge and o == -2:
                    slot = 7
                if not edge and o == 2:
                    slot = 8
                P.mm(bank[:, col:col + 128], K16[:, kb * 128:(kb + 1) * 128], Qb, True, False, reads=[tKb[hp_], tQb[hp_]], writes=[bt], ms=False)
                P.mm(bank[:, col:col + 128], ident, TM[:, slot * 128:(slot + 1) * 128], False, True, reads=[ct, tms[slot]], writes=[bt])
            for ci_, kb in enumerate((NLB, NLB + 1)):
                P.mm(bB[:, 256 + ci_ * 128:256 + (ci_ + 1) * 128], K16[:, kb * 128:(kb + 1) * 128], Qb, True, True, reads=[tKb[hp_], tQb[hp_]], writes=[bBt])
            na = min(nb, 4) * 128
            P.op("act", lambda e, bA=bA, E_=E_, na=na: e.activation(out=E_[:, 0:na], in_=bA[:, 0:na], func=AF.Exp), reads=[bAt], writes=[tE[0]])
            lo = 0 if nb == 5 else 256
            P.op("act", lambda e, bB=bB, E_=E_, lo=lo: e.activation(out=E_[:, 512 + lo:1024], in_=bB[:, lo:512], func=AF.Exp), reads=[bBt], writes=[tE[1]])

        def stB(t, h, qb):
            hp_ = h % 2
            V3 = Vb[hp_].rearrange("p (b c) -> p b c", c=128)
            _, _, (bO, bOt), (bD, bDt) = banks(t)
            kbs = kbs_of(qb)
            nb = len(kbs)
            E_, tE = E[t % 2], t_E[t % 2]
            slots = list(range(min(nb, 4))) + ([4] if nb == 5 else []) + [6, 7]
            allk = kbs + [NLB, NLB + 1]
            for si, (kb, sl) in enumerate(zip(allk, slots)):
                P.mm(bO[:, 0:128], V3[:, kb, :], E_[:, sl * 128:(sl + 1) * 128], si == 0, si == len(allk) - 1, reads=[tVb[hp_], tE[0], tE[1]], writes=[bOt])
            for si, (kb, sl) in enumerate(zip(allk, slots)):
                P.mm(bD[:, 0:128], self.ones16[:], E_[:, sl * 128:(sl + 1) * 128], si == 0, si == len(allk) - 1, reads=[ct, tE[0], tE[1]], writes=[bDt])
            dn_, tdn = den[t % 2], t_den[t % 2]
            OH = OHb[hp_]
            P.op("dve", lambda e, dn_=dn_, bD=bD: e.reciprocal(out=dn_, in_=bD[:, 0:128]), reads=[bDt], writes=[tdn])
            P.op("dve", lambda e, dn_=dn_, bO=bO, qb=qb, OH=OH: e.tensor_tensor(out=OH[:, qb * 128:(qb + 1) * 128], in0=bO[:, 0:128], in1=dn_, op=ALU.mult),
                 reads=[bOt, tdn], writes=[tOHb[hp_][qb]])
            if qb == NQB - 1:
                P.dma("pool", self.CAT[h * 128:(h + 1) * 128, 0:OWN], OH, reads=tOHb[hp_], writes=[t_CAT], partial=True, pn="pst")

        jobs = [(h, qb) for h in range(16) for qb in range(NQB)]
        prologue(0)
        for t in range(len(jobs) + 1):
            if t < len(jobs):
                h, qb = jobs[t]
                stA(t, h, qb)
            if t >= 1:
                stB(t - 1, *jobs[t - 1])
            if t < len(jobs) and jobs[t][1] == 0 and jobs[t][0] + 1 < 16:
                prologue(jobs[t][0] + 1)
        P.barrier()
        t_CAT.w = []; t_CAT.r = {}
        self.out_proj("na_out", l, with_ctx=False, nlat=NCH_OWN)

    def body(self):
        ch = self.chunks()

        def src0(c):
            t0, T, w = c
            return (self.ctxT[:, :] if w else self.xT[:, t0:t0 + T]), None

        def xt(c):
            t0, T, w = c
            return self.XT[:, t0:t0 + T], self.xt_tk(t0)

        if self.stop_after == "ffn00":
            def dst_dbg(c):
                t0, T, w = c
                if w:
                    return self.XT[:, t0:t0 + T], self.xt_tk(t0)
                return self.outT[:, t0:t0 + T], self.xt_tk(t0)
            self.ffn(0, 0, src0, dst_dbg)
            return
        self.ffn(0, 0, src0, xt, hook=self._mod0_hook)
        self.mixer_consts()
        self.mixer_ab()
        if self.stop_after == "mix0":
            P = self.P
            A32 = self.A32
            for ci, (t0, T, w) in enumerate(self.chunks(False, NCH_OWN)):
                A32.reset()
                x32 = A32.take(KC * 512); x3 = x32.rearrange("p (k t) -> p k t", t=512); t_x = P.tk()
                P.dma("sp", x3, self.XT[:, t0:t0 + T].rearrange("(k p) t -> p k t", p=128), writes=[t_x])
                P.dma("act", self.outT[:, t0:t0 + T].rearrange("(k p) t -> p k t", p=128), x3, reads=[t_x], writes=[self.xt_tk(ci)])
                P.barrier()
            return
        self.ffn(0, 1, xt, xt, nlat=NCH_EXT)
        self.ffn(1, 0, xt, xt, nlat=NCH_EXT)
        self.mixer_na()

        def dst_out(c):
            t0, T, w = c
            return self.outT[:, t0:t0 + T], self.xt_tk(t0)
        self.ffn(1, 1, xt, dst_out, with_ctx=False, nlat=NCH_OWN)


def _mixer_consts_host(inputs, mir):
    f = lambda a: np.ascontiguousarray(a, dtype=np.float32)
    L = SEQ
    pos = np.arange(L)
    if mir:
        pos = pos[::-1]
    prow, pcol = (pos // 64).astype(np.float32), (pos % 64).astype(np.float32)
    nf = 32
    inv = (np.float32(10000.0) ** (-np.arange(nf, dtype=np.float32) / np.float32(nf))).astype(np.float32)
    ar = (prow[None, :] * inv[:, None]).astype(np.float32)
    ac = (pcol[None, :] * inv[:, None]).astype(np.float32)
    cos = np.concatenate([np.cos(ar), np.cos(ar), np.cos(ac), np.cos(ac)], axis=0)
    sin = np.concatenate([np.sin(ar), np.sin(ar), np.sin(ac), np.sin(ac)], axis=0)
    ropeT = f(np.concatenate([cos, sin], axis=1))
    prot = np.zeros((128, 128), np.float32)
    for d in range(128):
        if (d % 64) < 32:
            prot[d + 32, d] = -1.0
        else:
            prot[d - 32, d] = 1.0
    j = np.arange(128)[:, None]
    i = np.arange(128)[None, :]
    mP = np.where(j >= i, 0.0, -BIG).astype(np.float32)
    mN = np.where(j <= i, 0.0, -BIG).astype(np.float32)
    maskPN = f(np.concatenate([np.tile(mP, (1, 4)), np.tile(mN, (1, 4))], axis=1))
    dsel = [1, 0] if mir else [0, 1]
    sp = np.zeros((128, NSP), np.float32)
    cw = inputs["lru_conv_w"][0]
    cw5 = np.zeros((5, 1024), np.float32)
    if mir:
        cw5[1:5] = cw[::-1]
    else:
        cw5[0:4] = cw
    sp[:, SP_CW:SP_CW + 40] = cw5.reshape(5, 8, 128).transpose(2, 1, 0).reshape(128, 40)
    sp[:, SP_CB:SP_CB + 8] = inputs["lru_conv_b"][0].reshape(8, 128).T
    sp[:, SP_BA:SP_BA + 16] = inputs["lru_b_a"][0][dsel].reshape(2, 8, 128).transpose(2, 0, 1).reshape(128, 16)
    sp[:, SP_BX:SP_BX + 16] = inputs["lru_b_x"][0][dsel].reshape(2, 8, 128).transpose(2, 0, 1).reshape(128, 16)
    sp[:, SP_LM:SP_LM + 16] = inputs["lru_lambda"][0][dsel].reshape(2, 8, 128).transpose(2, 0, 1).reshape(128, 16)
    sp[:, SP_QG] = inputs["attn_q_norm"][0]
    sp[:, SP_KG] = inputs["attn_k_norm"][0]
    sp[:, SP_SK:SP_SK + 8] = inputs["attn_sink"][0][None, :]
    sp[:, SP_NQG] = inputs["na_q_norm"][0]
    sp[:, SP_NKG] = inputs["na_k_norm"][0]
    wa = inputs["lru_w_a"][0][dsel]
    wx = inputs["lru_w_x"][0][dsel]
    gw = np.stack([wa.reshape(16, 128, 128), wx.reshape(16, 128, 128)], axis=0)
    gatew = f(gw.transpose(2, 0, 1, 3).reshape(128, 4096))
    rpb = inputs["na_rpb"][0]
    if mir:
        rpb = rpb[:, ::-1, ::-1]
    kc = np.arange(64)[:, None]
    qc = np.arange(64)[None, :]
    bidx = np.clip(kc - qc + 15, 0, 30)
    natt = np.zeros((128, 16, 15, 64), np.float32)
    for idx in range(15):
        natt[0:64, :, idx, :] = rpb[:, 14 - idx, :][:, bidx].transpose(1, 0, 2)
        if idx >= 1:
            natt[64:128, :, idx, :] = rpb[:, 15 - idx, :][:, bidx].transpose(1, 0, 2)
    ws = np.clip(np.arange(64) - (7 if mir else 8), 0, 48)[None, :]
    colv = (kc >= ws) & (kc < ws + 16)
    cv = np.tile(colv, (2, 2))
    khh = (np.arange(128) // 64)[:, None]
    qhh = (np.arange(128) // 64)[None, :]
    m_all = np.where(cv, 0.0, -BIG)
    if mir:
        m_m2 = np.where(cv & ((khh == 1) & (qhh == 0)), 0.0, -BIG)
        m_p2 = np.where(cv & ~((khh == 1) & (qhh == 0)), 0.0, -BIG)
    else:
        m_m2 = np.where(cv & ~((khh == 0) & (qhh == 1)), 0.0, -BIG)
        m_p2 = np.where(cv & ((khh == 0) & (qhh == 1)), 0.0, -BIG)
    namask = f(np.concatenate([m_all, m_m2, m_p2], axis=1))
    prot = f(np.concatenate([prot, np.eye(128, dtype=np.float32)], axis=1))
    return {"ropeT": ropeT, "prot": prot, "maskPN": maskPN, "smallp": sp, "gatew": gatew,
            "natt": f(natt.reshape(128, 16 * 960)), "namask": namask}


def _core_cfg(r):
    if HALF:
        return r // 2, (r % 2) == 1
    return r, (MIRROR_TEST and (r % 2) == 1)


def _host_inputs(inputs, names=None):
    f = lambda a: np.ascontiguousarray(a, dtype=np.float32)
    x, c, ctx, c_ctx = inputs["x"], inputs["c"], inputs["ctx"], inputs["c_ctx"]
    bmod = f(inputs["b_mod"].reshape(2, 144, 128).transpose(2, 0, 1).reshape(128, 2 * 144))
    normg = f(inputs["norm_g"].reshape(2, 3, KC, 128).transpose(3, 0, 1, 2).reshape(128, 2 * 3 * KC))
    wfull = {
        "f_in_00": inputs["ffn_w_in"][0, 0], "f_out_00": inputs["ffn_w_out"][0, 0],
        "f_in_01": inputs["ffn_w_in"][0, 1], "f_out_01": inputs["ffn_w_out"][0, 1],
        "f_in_10": inputs["ffn_w_in"][1, 0], "f_out_10": inputs["ffn_w_out"][1, 0],
        "f_in_11": inputs["ffn_w_in"][1, 1], "f_out_11": inputs["ffn_w_out"][1, 1],
        "ab_in": inputs["ab_w_in"][0], "ab_out": inputs["ab_w_out"][0],
        "na_in": inputs["na_w_in"][0], "na_out": inputs["na_w_out"][0],
    }
    shared = {"wmod": f(inputs["w_mod"]), "bmod": bmod, "normg": normg}
    for name, rows, cols in WSPEC:
        if names is None or ("w_" + name) in names:
            shared["w_" + name] = f(wfull[name])
    mc = {}
    maps = []
    for r in range(NCORES):
        b, mir = _core_cfg(r)
        if mir not in mc:
            mc[mir] = _mixer_consts_host(inputs, mir)
        m = dict(shared)
        m.update(mc[mir])
        xb, cb = x[b], ctx[b]
        if mir:
            xb, cb = xb[::-1], cb[::-1]
        m["xT"] = f(xb.T)
        m["ctxT"] = f(cb.T)
        crows = np.stack([c[b], c_ctx], axis=0)
        m["cT"] = f(crows.reshape(2, KC, 128).transpose(2, 1, 0).reshape(128, KC * 2))
        if names is not None:
            m = {k: v for k, v in m.items() if k in names}
        maps.append(m)
    return maps


_NC_CACHE = {}


def kernel(**inputs):
    key = "full"
    if key not in _NC_CACHE:
        _NC_CACHE[key] = Builder().build()
    nc = _NC_CACHE[key]
    maps = _host_inputs(inputs)
    res = run_bass_kernel_spmd(nc, maps, core_ids=list(range(NCORES)))
    out = np.empty((4, 4096, D), np.float32)
    for r in range(NCORES):
        b, mir = _core_cfg(r)
        o = res.results[r]["outT"].T
        if HALF:
            if mir:
                out[b, SEQ - OWN:SEQ] = o[::-1]
            else:
                out[b, 0:OWN] = o
        else:
            out[b] = o[::-1] if mir else o
    return out
```

```python
from contextlib import ExitStack
import numpy as np
import concourse.bass as bass
import concourse.mybir as mybir
from concourse.bass_utils import run_bass_kernel_spmd

F32 = mybir.dt.float32
BF16 = mybir.dt.bfloat16
AF = mybir.ActivationFunctionType
ALU = mybir.AluOpType

D = 2048
KC = 16
DFF = 5504
FC = 43
SEQ = 4096
LAT = 4096
CTX = 256
HALF = True
MIRROR_TEST = True
NCORES = 8 if HALF else 4
OWN = 2048 if HALF else 4096
NCH_ALL = 8
NCH_EXT = 5 if HALF else 8
NCH_OWN = OWN // 512
EPS = 1e-6
BIG = 30000.0
NT = LAT + CTX
NSP = 108
SP_CW, SP_CB, SP_BA, SP_BX, SP_LM, SP_QG, SP_KG, SP_SK, SP_NQG, SP_NKG = 0, 40, 48, 64, 80, 96, 97, 98, 106, 107
C_SP, C_MASK, C_SP8, C_GQS, C_ESINK, C_NQS = 0, 108, 1132, 1148, 1149, 1157


class Tk:
    __slots__ = ("name", "w", "r")

    def __init__(self, name=""):
        self.name = name
        self.w = []
        self.r = {}


class Stream:
    def __init__(self, name, sem):
        self.name = name
        self.sem = sem
        self.items = []
        self.n = 0
        self.pending = False
        self.seen = {}


class Prog:
    NP = 12

    def __init__(self, nc, es):
        self.nc = nc
        self.streams = {n: Stream(n, es.enter_context(nc.semaphore("s_" + n)))
                        for n in ("pe", "act", "dve", "pool", "sp")}
        self.dpool = {n: [es.enter_context(nc.semaphore(f"d_{n}{i}")) for i in range(np_)]
                      for n, np_ in (("sp", self.NP), ("act", self.NP), ("pool", self.NP), ("pst", self.NP), ("cast", 5))}
        self.dnext = {n: 0 for n in self.dpool}
        self.dtot = {}
        self.phase_tiles = []
        self.es = es
        self.ncc = 0

    def tk(self, name="", phase=True):
        t = Tk(name)
        if phase:
            self.phase_tiles.append(t)
        return t

    def _wait(self, S, ev):
        kind, key, val = ev
        if kind == "eng":
            if key == S.name:
                if key == "pe" or val <= S.n - 3:
                    return
            k = key
            sem = self.streams[key].sem
        else:
            k = id(key)
            sem = key
        if S.seen.get(k, 0) >= val:
            return
        S.seen[k] = val
        S.items.append(("wait", sem, val))

    def _deps(self, S, reads, writes, partial):
        for t in reads:
            for ev in t.w:
                self._wait(S, ev)
        for t in writes:
            if not partial:
                for ev in t.w:
                    self._wait(S, ev)
            for ev in t.r.values():
                self._wait(S, ev)

    def op(self, sname, fn, reads=(), writes=(), ms=True):
        S = self.streams[sname]
        self._deps(S, reads, writes, False)
        if ms:
            S.n += 1
            tick = S.n
            S.items.append(("op", fn, S.sem, 1))
        else:
            tick = S.n + 1
            S.items.append(("op", fn, None, 0))
        S.pending = not ms
        ev = ("eng", sname, tick)
        for t in writes:
            t.w = [ev]
            t.r = {}
        for t in reads:
            t.r[sname] = ev

    def dma(self, sname, out, in_, reads=(), writes=(), partial=False, pn=None):
        S = self.streams[sname]
        self._deps(S, reads, writes, partial)
        pn = pn or sname
        pool = self.dpool[pn]
        i = self.dnext[pn]
        self.dnext[pn] = (i + 1) % len(pool)
        sem = pool[i]
        tot = self.dtot.get(id(sem), 0)
        if tot:
            self._wait(S, ("dma", sem, tot))
        tot += 16
        self.dtot[id(sem)] = tot
        S.items.append(("op", lambda e, out=out, in_=in_: e.dma_start(out=out, in_=in_), sem, 16))
        ev = ("dma", sem, tot)
        for t in writes:
            if partial:
                t.w.append(ev)
            else:
                t.w = [ev]
                t.r = {}
        for t in reads:
            t.r[("d", id(sem))] = ev

    def allgather(self, groups, src, dst, reads=(), writes=()):
        S = self.streams["pool"]
        self._deps(S, reads, writes, False)
        sem = self.es.enter_context(self.nc.semaphore(f"cc{self.ncc}"))
        self.ncc += 1
        S.items.append(("op", lambda e: e.collective_compute(
            "AllGather", ALU.bypass, replica_groups=groups, ins=[src], outs=[dst]), sem, -1))
        ev = ("dma", sem, 1)
        for t in writes:
            t.w = [ev]
            t.r = {}
        for t in reads:
            t.r[("d", id(sem))] = ev

    def mm(self, out, lhsT, rhs, start, stop, reads, writes, ms=None):
        self.op("pe", lambda e: e.matmul(out, lhsT, rhs, start=start, stop=stop),
                reads=reads, writes=writes, ms=stop if ms is None else ms)

    def barrier(self):
        names = ("pe", "act", "dve", "sp")
        evs = []
        for n in ("pe", "act", "dve"):
            S = self.streams[n]
            assert not S.pending, n
            if S.n:
                evs.append(("eng", n, S.n))
        for n in ("sp", "act", "pst"):
            for sem in self.dpool[n]:
                tot = self.dtot.get(id(sem), 0)
                if tot:
                    evs.append(("dma", sem, tot))
        for n in names:
            for ev in evs:
                self._wait(self.streams[n], ev)
        for t in self.phase_tiles:
            t.w = []
            t.r = {}
        self.phase_tiles = []

    @staticmethod
    def _run(items, e):
        for it in items:
            if it[0] == "wait":
                e.wait_ge(it[1], it[2])
            else:
                ins = it[1](e)
                if it[2] is not None:
                    if it[3] == 1 and False:
                        ins.then_inc(it[2])
                    else:
                        ins.then_inc(it[2], it[3]) if it[3] != -1 else ins.then_inc(it[2])

    def check_deadlock(self):
        val = {}
        pc = {n: 0 for n in self.streams}
        progress = True
        while progress:
            progress = False
            for n, S in self.streams.items():
                while pc[n] < len(S.items):
                    it = S.items[pc[n]]
                    if it[0] == "wait":
                        if val.get(id(it[1]), 0) >= it[2]:
                            pc[n] += 1; progress = True
                        else:
                            break
                    else:
                        if it[2] is not None:
                            val[id(it[2])] = val.get(id(it[2]), 0) + abs(it[3])
                        pc[n] += 1; progress = True
        bad = {n: (pc[n], len(S.items)) for n, S in self.streams.items() if pc[n] < len(S.items)}
        if bad:
            for n in bad:
                it = self.streams[n].items[pc[n]]
                print("DEADLOCK", n, bad[n], it[0], getattr(it[1], "name", it[1]), it[2], "cur", val.get(id(it[1]), 0))
        return not bad

    def emit(self):
        nc = self.nc
        assert self.check_deadlock()
        print("stream lens", {n: len(S.items) for n, S in self.streams.items()})
        with nc.Block() as block:
            block.tensor(lambda e: self._run(self.streams["pe"].items, e))
            block.scalar(lambda e: self._run(self.streams["act"].items, e))
            block.vector(lambda e: self._run(self.streams["dve"].items, e))
            block.gpsimd(lambda e: self._run(self.streams["pool"].items, e))
            block.sync(lambda e: self._run(self.streams["sp"].items, e))


class Arena:
    def __init__(self, t, size):
        self.t = t
        self.size = size
        self.off = 0

    def reset(self):
        self.off = 0

    def take(self, n):
        assert self.off + n <= self.size, (self.off, n, self.size)
        a = self.t[:, self.off:self.off + n]
        self.off += n
        return a


WSPEC = [
    ("f_in_00", D, 2 * DFF), ("f_out_00", DFF, D),
    ("ab_in", D, 3584), ("ab_out", D, D),
    ("f_in_01", D, 2 * DFF), ("f_out_01", DFF, D),
    ("f_in_10", D, 2 * DFF), ("f_out_10", DFF, D),
    ("na_in", D, 6144), ("na_out", D, D),
    ("f_in_11", D, 2 * DFF), ("f_out_11", DFF, D),
]


class Builder:
    def __init__(self, stop_after=None, debug=()):
        self.stop_after = stop_after
        self.debug = debug
        self.nc = bass.Bass("TRN2", target_bir_lowering=False)

    def ext_in(self, name, shape, dt=F32):
        return self.nc.dram_tensor(name, list(shape), dt, kind="ExternalInput").ap()

    def ext_out(self, name, shape, dt=F32):
        return self.nc.dram_tensor(name, list(shape), dt, kind="ExternalOutput").ap()

    def dram(self, name, shape, dt=F32):
        return self.nc.dram_tensor(name, list(shape), dt).ap()

    def build(self):
        nc = self.nc
        with ExitStack() as es:
            self.es = es
            P = self.P = Prog(nc, es)
            sb = lambda name, shape, dt: es.enter_context(nc.sbuf_tensor(name, list(shape), dt))
            self.ring16 = sb("ring16", [128, 6, KC * 128], BF16)
            self.ring16_t = [P.tk(f"r16_{i}", phase=False) for i in range(6)]
            self.ring43 = sb("ring43", [128, 2, FC * 128], BF16)
            self.ring43_t = [P.tk(f"r43_{i}", phase=False) for i in range(2)]
            self.r16n = 0
            self.r43n = 0
            self.ones32 = sb("ones32", [128, 128], F32)
            self.ones16 = sb("ones16", [128, 128], BF16)
            self.epsc = sb("epsc", [128, 2], F32)
            self.modtab = sb("modtab", [128, 2, 2, 9 * KC], F32)
            self.consts_t = P.tk("consts", phase=False)
            a16 = sb("a16", [128, 38400], BF16)
            a32 = sb("a32", [128, 16000], F32)
            self.A16 = Arena(a16, 38400)
            self.A32 = Arena(a32, 16000)
            self.cst32 = sb("cst32", [128, 1200], F32)
            self.modc32 = sb("modc32", [128, 1024], F32)
            self.modc16 = sb("modc16", [128, 32], BF16)
            self.cst16 = sb("cst16", [128, 256 + 4096], BF16)
            self.psum = [es.enter_context(nc.psum_tensor(f"ps{i}", [128, 512], F32)) for i in range(8)]
            self.psum_t = [P.tk(f"ps{i}", phase=False) for i in range(8)]

            self.declare_io()
            self.init_consts()
            self.prep_weights()
            self.P.barrier()
            self.body()
            P.barrier()
            P.emit()
        return nc

    def declare_io(self):
        self.xT = self.ext_in("xT", [D, LAT])
        self.ctxT = self.ext_in("ctxT", [D, CTX])
        self.cT = self.ext_in("cT", [128, KC * 2])
        self.wmod = self.ext_in("wmod", [2, D, 9 * D])
        self.bmod = self.ext_in("bmod", [128, 2 * 144])
        self.normg = self.ext_in("normg", [128, 2 * 3 * KC])
        self.wext = {}
        for name, rows, cols in WSPEC:
            if self.stop_after in ("ffn00", "mod", "prep") and name not in ("f_in_00", "f_out_00"):
                continue
            if self.stop_after == "mix0" and name not in ("f_in_00", "f_out_00", "ab_in", "ab_out"):
                continue
            self.wext[name] = self.ext_in("w_" + name, [rows, cols])
        full = self.stop_after not in ("ffn00", "mod", "prep")
        self.full = full
        if full:
            self.ropeT = self.ext_in("ropeT", [128, 2 * LAT])
            self.prot = self.ext_in("prot", [128, 256])
            self.maskPN = self.ext_in("maskPN", [128, 1024])
            self.smallp = self.ext_in("smallp", [128, NSP])
            self.gatew = self.ext_in("gatew", [128, 4096])
            mk = self.ext_out if "dbg" in self.debug else self.dram
            self.XL = mk("XL", [1024, NT])
            self.GG = mk("GG", [1024, NT])
            self.HP = mk("HP", [1024, NT])
            self.QT = mk("QT", [1024, NT], BF16)
            self.KT = mk("KT", [256, NT], BF16)
            self.VV = mk("VV", [NT, 256], BF16)
            self.CAT = mk("CAT", [D, NT], BF16)
            self.natt = self.ext_in("natt", [128, 16 * 960])
            self.namask = self.ext_in("namask", [128, 384])
            self.QT1 = self.dram("QT1", [D, NT], BF16)
            self.KT1 = self.dram("KT1", [D, NT], BF16)
            self.VV1 = self.dram("VV1", [NT, D], BF16)
        self.outT = self.ext_out("outT", [D, OWN])
        self.XT = self.dram("XT", [D, LAT + CTX])
        self.XT_t = {}

    def xt_tk(self, c):
        if c not in self.XT_t or self.XT_t[c] not in self.P.phase_tiles:
            self.XT_t[c] = self.P.tk(f"XT{c}")
        return self.XT_t[c]

    def prep_B(self, name):
        P = self.P
        rows, cols = self.wdims[name]
        kc = rows // 128
        nt = cols // 128
        full = self.wext[name]
        w16 = self.dram("wb_" + name, [nt, 128, kc * 128], BF16)
        fv = full.rearrange("(kc p) n -> p kc n", p=128)
        tks = []
        for n in range(nt):
            t = P.tk(phase=False)
            P.dma("pool", w16[n].rearrange("p (kc n) -> p kc n", n=128), fv[:, :, n * 128:(n + 1) * 128], writes=[t], pn="cast")
            tks.append(t)
        self.w16[name] = w16
        self.w16_t[name] = tks

    def prep_weights(self):
        self.w16 = {}
        self.w16_t = {}
        self.wdims = {n: (r, c) for n, r, c in WSPEC}
        names = [n for n, _, _ in WSPEC]
        if self.stop_after in ("ffn00", "mod", "prep"):
            names = names[:2]
        if self.stop_after == "mix0":
            names = names[:4]
        if self.stop_after != "prep":
            self.modulation()
        for n in names:
            self.prep_B(n)

    def wtile(self, name, n):
        P = self.P
        w16 = self.w16[name]
        kc = w16.shape[2] // 128
        if kc == KC:
            i = self.r16n % 6
            self.r16n += 1
            slot, t = self.ring16[:, i, :], self.ring16_t[i]
        else:
            i = self.r43n % 2
            self.r43n += 1
            slot, t = self.ring43[:, i, :], self.ring43_t[i]
        P.dma("sp", slot, w16[n], reads=[self.w16_t[name][n]], writes=[t])
        return slot.rearrange("p (kc n) -> p kc n", n=128), t

    def init_consts(self):
        P = self.P
        P.op("dve", lambda e: e.memset(self.ones32[:], 1.0), writes=[self.consts_t])
        P.op("dve", lambda e: e.memset(self.ones16[:], 1.0), writes=[self.consts_t])
        P.op("dve", lambda e: e.memset(self.epsc[:], EPS), writes=[self.consts_t])

    def mod_setup(self):
        P = self.P
        mc = self.modc32
        self.m_c32 = mc[:, 0:32]; self.m_sig = mc[:, 32:64]; self.m_bm = mc[:, 64:352]; self.m_ng = mc[:, 352:448]; self.m_msel = mc[:, 448:736]
        self.m_sc16 = self.modc16[:, 0:32]
        self.t_mc = P.tk(phase=False); self.t_sc = P.tk(phase=False); self.t_msel = P.tk(phase=False)
        P.dma("sp", self.m_c32, self.cT[:, :], writes=[self.t_mc])
        P.dma("sp", self.m_bm, self.bmod[:, :], writes=[self.t_mc], partial=True)
        P.dma("sp", self.m_ng, self.normg[:, :], writes=[self.t_mc], partial=True)
        P.op("act", lambda e: e.activation(out=self.m_sig, in_=self.m_c32, func=AF.Sigmoid), reads=[self.t_mc], writes=[self.t_sc])
        P.op("dve", lambda e: e.tensor_tensor(out=self.m_sc16, in0=self.m_sig, in1=self.m_c32, op=ALU.mult), reads=[self.t_sc, self.t_mc], writes=[self.t_sc])

    def mod_tiles(self, l, gcs, bank=0):
        P = self.P
        sc3 = self.m_sc16.rearrange("p (kc r) -> p kc r", r=2)
        ps, pst = self.psum[bank], self.psum_t[bank]
        for gc in gcs:
            i = self.r16n % 6
            self.r16n += 1
            slot, t = self.ring16[:, i, :], self.ring16_t[i]
            src = self.wmod[l].rearrange("(kc p) n -> p kc n", p=128)[:, :, gc * 128:(gc + 1) * 128]
            s3 = slot.rearrange("p (kc n) -> p kc n", n=128)
            P.dma("pool", s3, src, writes=[t])
            for kc in range(KC):
                P.mm(ps[:, 2 * gc:2 * gc + 2], s3[:, kc, :], sc3[:, kc, :], kc == 0, kc == KC - 1, reads=[t, self.t_sc], writes=[pst])

    def mod_finish(self, l, bank=0):
        P = self.P
        ps, pst = self.psum[bank], self.psum_t[bank]
        ms4 = self.m_msel.rearrange("p (g w) -> p g w", g=144, w=2)
        bm3 = self.m_bm.rearrange("p (l g) -> p l g", l=2)
        ng4 = self.m_ng.rearrange("p (l s k) -> p l s k", l=2, s=3)
        t_msel, t_mc = self.t_msel, self.t_mc
        for w in range(2):
            P.op("dve", lambda e, d=ms4[:, :, w], p_=ps[:, 0:288].rearrange("p (g w) -> p g w", w=2)[:, :, w], b=bm3[:, l, :]:
                 e.tensor_tensor(out=d, in0=p_, in1=b, op=ALU.add), reads=[pst, t_mc], writes=[t_msel])
            for s in range(3):
                shift = ms4[:, (3 * s) * KC:(3 * s + 1) * KC, w]
                scale = ms4[:, (3 * s + 1) * KC:(3 * s + 2) * KC, w]
                gate = ms4[:, (3 * s + 2) * KC:(3 * s + 3) * KC, w]
                Ad = self.modtab[:, l, w, (3 * s) * KC:(3 * s + 1) * KC]
                Bd = self.modtab[:, l, w, (3 * s + 1) * KC:(3 * s + 2) * KC]
                Gd = self.modtab[:, l, w, (3 * s + 2) * KC:(3 * s + 3) * KC]
                P.op("dve", lambda e, d=Ad, sc=scale, g=ng4[:, l, s, :]: e.scalar_tensor_tensor(
                    out=d, in0=sc, scalar=1.0, in1=g, op0=ALU.add, op1=ALU.mult), reads=[t_msel, t_mc], writes=[self.consts_t])
                P.op("dve", lambda e, d=Bd, sh=shift: e.tensor_copy(out=d, in_=sh), reads=[t_msel], writes=[self.consts_t])
                f = 0.5 if s != 1 else 1.0
                P.op("dve", lambda e, d=Gd, g=gate, f=f: e.tensor_scalar(out=d, in0=g, scalar1=f, scalar2=None, op0=ALU.mult),
                     reads=[t_msel], writes=[self.consts_t])

    def modulation(self):
        self.mod_setup()
        self.mod_tiles(0, range(144))
        self.mod_finish(0)
        if self.stop_after in ("mod", "ffn00"):
            self.mod_tiles(1, range(144), bank=1)
            self.mod_finish(1, bank=1)

    def mod(self, l, w, s, which):
        k = {"A": 0, "B": 1, "G": 2}[which]
        return self.modtab[:, l, w, (3 * s + k) * KC:(3 * s + k + 1) * KC]

    def norm_mod(self, x3, t_x, h3, t_h, T, l, w, s):
        P = self.P
        A32 = self.A32
        sq = [A32.take(512), A32.take(512)]
        t_sq = [P.tk(), P.tk()]
        rstd = A32.take(512)
        t_rstd = P.tk()
        ps, pst = self.psum[0], self.psum_t[0]
        for kc in range(KC):
            P.op("act", lambda e, o=sq[kc % 2][:, :T], i=x3[:, kc, :]: e.activation(out=o, in_=i, func=AF.Square),
                 reads=[t_x], writes=[t_sq[kc % 2]])
            P.mm(ps[:, :T], self.ones32[:], sq[kc % 2][:, :T], kc == 0, kc == KC - 1, reads=[t_sq[kc % 2], self.consts_t], writes=[pst], ms=True)
        P.op("act", lambda e: e.activation(out=rstd[:, :T], in_=ps[:, :T], func=AF.Sqrt, bias=self.epsc[:, 0:1], scale=1.0 / D),
             reads=[pst, self.consts_t], writes=[t_rstd])
        P.op("dve", lambda e: e.reciprocal(out=rstd[:, :T], in_=rstd[:, :T]), reads=[t_rstd], writes=[t_rstd])
        Am = self.mod(l, w, s, "A")
        Bm = self.mod(l, w, s, "B")
        for kc in range(KC):
            P.op("dve", lambda e, o=sq[kc % 2][:, :T], i=x3[:, kc, :], a=Am[:, kc:kc + 1]: e.scalar_tensor_tensor(
                out=o, in0=i, scalar=a, in1=rstd[:, :T], op0=ALU.mult, op1=ALU.mult),
                reads=[t_x, t_rstd, self.consts_t], writes=[t_sq[kc % 2]])
            P.op("act", lambda e, o=h3[:, kc, :], i=sq[kc % 2][:, :T], b=Bm[:, kc:kc + 1]: e.activation(
                out=o, in_=i, func=AF.Identity, bias=b, scale=1.0), reads=[t_sq[kc % 2], self.consts_t], writes=[t_h])

    def chunks(self, with_ctx=True, nlat=NCH_ALL):
        ch = [(i * 512, 512, 0) for i in range(nlat)]
        if with_ctx:
            ch.append((LAT, CTX, 1))
        return ch

    def ffn(self, l, s, src_fn, dst_fn, with_ctx=True, nlat=NCH_ALL):
        P = self.P
        win = f"f_in_{l}{s}"
        wout = f"f_out_{l}{s}"
        sub = 0 if s == 0 else 2
        chs = self.chunks(with_ctx, nlat)
        A32, A16 = self.A32, self.A16
        A32.reset(); A16.reset()
        x32 = A32.take(KC * 512); t_x = P.tk()
        h16 = [A16.take(KC * 512) for _ in range(2)]; t_h = [P.tk() for _ in range(2)]
        act16 = A16.take(FC * 512); t_a = [P.tk() for _ in range(FC)]
        sq = [A32.take(512) for _ in range(4)]; t_sq = [P.tk() for _ in range(4)]
        rstd = A32.take(512); t_rstd = P.tk()
        sil = [A32.take(512) for _ in range(2)]; t_sil = [P.tk() for _ in range(2)]
        yo = [A32.take(512) for _ in range(2)]; t_yo = [P.tk() for _ in range(2)]
        xr = [A32.take(512) for _ in range(2)]; t_xr = [P.tk() for _ in range(2)]
        pss, psst = self.psum[0], self.psum_t[0]

        def views(c):
            t0, T, w = chs[c]
            x3 = x32[:, :KC * T].rearrange("p (k t) -> p k t", t=T)
            h3 = h16[c % 2][:, :KC * T].rearrange("p (k t) -> p k t", t=T)
            return t0, T, w, x3, h3

        def load_x(c):
            t0, T, w, x3, h3 = views(c)
            src, t_src = src_fn((t0, T, w))
            P.dma("act", x3, src.rearrange("(k p) t -> p k t", p=128), reads=[t_src] if t_src else [], writes=[t_x])

        def square(c, kc):
            t0, T, w, x3, h3 = views(c)
            P.op("act", lambda e, o=sq[kc % 4][:, :T], i=x3[:, kc, :]: e.activation(out=o, in_=i, func=AF.Square), reads=[t_x], writes=[t_sq[kc % 4]])

        def onesmm(c, kc):
            t0, T, w, x3, h3 = views(c)
            P.mm(pss[:, :T], self.ones32[:], sq[kc % 4][:, :T], kc == 0, kc == KC - 1, reads=[t_sq[kc % 4], self.consts_t], writes=[psst], ms=True)

        def normB(c):
            t0, T, w, x3, h3 = views(c)
            P.op("act", lambda e: e.activation(out=rstd[:, :T], in_=pss[:, :T], func=AF.Sqrt, bias=self.epsc[:, 0:1], scale=1.0 / D), reads=[psst, self.consts_t], writes=[t_rstd])
            P.op("dve", lambda e: e.reciprocal(out=rstd[:, :T], in_=rstd[:, :T]), reads=[t_rstd], writes=[t_rstd])
            Am = self.mod(l, w, sub, "A")
            Bm = self.mod(l, w, sub, "B")
            for kc in range(KC):
                P.op("dve", lambda e, o=sq[kc % 4][:, :T], i=x3[:, kc, :], a=Am[:, kc:kc + 1]: e.scalar_tensor_tensor(
                    out=o, in0=i, scalar=a, in1=rstd[:, :T], op0=ALU.mult, op1=ALU.mult), reads=[t_x, t_rstd, self.consts_t], writes=[t_sq[kc % 4]])
                P.op("act", lambda e, o=h3[:, kc, :], i=sq[kc % 4][:, :T], b=Bm[:, kc:kc + 1]: e.activation(
                    out=o, in_=i, func=AF.Identity, bias=b, scale=1.0), reads=[t_sq[kc % 4], self.consts_t], writes=[t_h[c % 2]])

        load_x(0)
        for kc in range(KC):
            square(0, kc)
            onesmm(0, kc)
        normB(0)
        for c in range(len(chs)):
            t0, T, w, x3, h3 = views(c)
            th = t_h[c % 2]
            a3 = act16[:, :FC * T].rearrange("p (k t) -> p k t", t=T)
            src, t_src = src_fn((t0, T, w))
            dst, t_dst = dst_fn((t0, T, w))
            src3 = src.rearrange("(k p) t -> p k t", p=128)
            dst3 = dst.rearrange("(k p) t -> p k t", p=128)
            if c + 1 < len(chs):
                load_x(c + 1)
            for j in range(FC):
                wg, tg = self.wtile(win, j)
                pg, pgt = self.psum[1 + j % 2], self.psum_t[1 + j % 2]
                for kc in range(KC):
                    P.mm(pg[:, :T], wg[:, kc, :], h3[:, kc, :], kc == 0, kc == KC - 1, reads=[tg, th], writes=[pgt])
                wu, tu = self.wtile(win, FC + j)
                pu, put = self.psum[3 + j % 2], self.psum_t[3 + j % 2]
                for kc in range(KC):
                    P.mm(pu[:, :T], wu[:, kc, :], h3[:, kc, :], kc == 0, kc == KC - 1, reads=[tu, th], writes=[put])
                P.op("act", lambda e, o=sil[j % 2][:, :T], i=pg[:, :T]: e.activation(out=o, in_=i, func=AF.Silu), reads=[pgt], writes=[t_sil[j % 2]])
                P.op("dve", lambda e, o=a3[:, j, :], a=sil[j % 2][:, :T], b=pu[:, :T]: e.tensor_tensor(out=o, in0=a, in1=b, op=ALU.mult),
                     reads=[t_sil[j % 2], put], writes=[t_a[j]])
            Gm = self.mod(l, w, sub, "G")
            nxt = c + 1 < len(chs)
            for i in range(KC):
                P.dma("sp", xr[i % 2][:, :T], src3[:, i, :], reads=[], writes=[t_xr[i % 2]])
                wo, to = self.wtile(wout, i)
                py, pyt = self.psum[5 + i % 2], self.psum_t[5 + i % 2]
                for j in range(FC):
                    P.mm(py[:, :T], wo[:, j, :], a3[:, j, :], j == 0, j == FC - 1, reads=[to, t_a[j]], writes=[pyt])
                if nxt and 1 <= i <= 8:
                    onesmm(c + 1, 2 * (i - 1))
                    onesmm(c + 1, 2 * (i - 1) + 1)
                P.op("dve", lambda e, o=yo[i % 2][:, :T], y=py[:, :T], g=Gm[:, i:i + 1], x=xr[i % 2][:, :T]: e.scalar_tensor_tensor(
                    out=o, in0=y, scalar=g, in1=x, op0=ALU.mult, op1=ALU.add), reads=[pyt, t_xr[i % 2], self.consts_t], writes=[t_yo[i % 2]])
                P.dma("act", dst3[:, i, :], yo[i % 2][:, :T], reads=[t_yo[i % 2]], writes=[t_dst], partial=True)
                if nxt:
                    if i <= 7:
                        square(c + 1, 2 * i)
                        square(c + 1, 2 * i + 1)
                    if i == 9:
                        normB(c + 1)
        P.barrier()

    def mixer_consts(self):
        P = self.P
        A32 = self.A32
        A32.reset()
        c32, c16, ct = self.cst32, self.cst16, self.consts_t
        t1 = P.tk(); t2 = P.tk(); t3 = P.tk()
        P.dma("sp", c32[:, C_SP:C_SP + NSP], self.smallp[:, :], writes=[ct])
        P.dma("sp", c32[:, C_MASK:C_MASK + 1024], self.maskPN[:, :], writes=[ct], partial=True)
        pr = A32.take(256)
        P.dma("sp", pr, self.prot[:, :], writes=[t1])
        P.op("dve", lambda e: e.tensor_copy(out=c16[:, 0:256], in_=pr), reads=[t1], writes=[ct])
        for hf in range(2):
            gw = A32.take(2048)
            tg = P.tk()
            P.dma("sp", gw, self.gatew[:, hf * 2048:(hf + 1) * 2048], writes=[tg])
            P.op("dve", lambda e, gw=gw, hf=hf: e.tensor_copy(out=c16[:, 256 + hf * 2048:256 + (hf + 1) * 2048], in_=gw), reads=[tg], writes=[ct])
        sp8 = c32[:, C_SP8:C_SP8 + 16]
        ev = A32.take(16); l1 = A32.take(16); ec = A32.take(16); pl = A32.take(16); mk_ = A32.take(16)
        tq = P.tk()
        P.op("act", lambda e: e.activation(out=ev, in_=c32[:, SP_LM:SP_LM + 16], func=AF.Exp, scale=-1.0), reads=[ct], writes=[tq])
        P.op("act", lambda e: e.activation(out=l1, in_=ev, func=AF.Ln, bias=self.ones32[:, 0:1], scale=1.0), reads=[tq, ct], writes=[tq])
        P.op("dve", lambda e: e.tensor_scalar(out=ec, in0=ev, scalar1=0.05, scalar2=None, op0=ALU.min), reads=[tq], writes=[tq])
        P.op("dve", lambda e: e.tensor_scalar(out=pl, in0=ec, scalar1=0.2, scalar2=-0.25, op0=ALU.mult, op1=ALU.add), reads=[tq], writes=[tq])
        for cf in (1.0 / 3, -0.5, 1.0):
            P.op("dve", lambda e: e.tensor_tensor(out=pl, in0=pl, in1=ec, op=ALU.mult), reads=[tq], writes=[tq])
            P.op("dve", lambda e, cf=cf: e.tensor_scalar(out=pl, in0=pl, scalar1=float(cf), scalar2=None, op0=ALU.add), reads=[tq], writes=[tq])
        P.op("dve", lambda e: e.tensor_tensor(out=pl, in0=pl, in1=ec, op=ALU.mult), reads=[tq], writes=[tq])
        P.op("dve", lambda e: e.tensor_scalar(out=mk_, in0=ev, scalar1=0.05, scalar2=None, op0=ALU.is_gt), reads=[tq], writes=[tq])
        P.op("dve", lambda e: e.tensor_tensor(out=l1, in0=l1, in1=pl, op=ALU.subtract), reads=[tq], writes=[tq])
        P.op("dve", lambda e: e.tensor_tensor(out=l1, in0=l1, in1=mk_, op=ALU.mult), reads=[tq], writes=[tq])
        P.op("dve", lambda e: e.tensor_tensor(out=l1, in0=l1, in1=pl, op=ALU.add), reads=[tq], writes=[tq])
        P.op("dve", lambda e: e.tensor_scalar(out=sp8, in0=l1, scalar1=-8.0, scalar2=None, op0=ALU.mult), reads=[tq], writes=[ct])
        P.op("dve", lambda e: e.tensor_scalar(out=c32[:, C_GQS:C_GQS + 1], in0=c32[:, SP_QG:SP_QG + 1], scalar1=float(128 ** -0.5), scalar2=None, op0=ALU.mult), reads=[ct], writes=[ct])
        P.op("dve", lambda e: e.tensor_scalar(out=c32[:, C_NQS:C_NQS + 1], in0=c32[:, SP_NQG:SP_NQG + 1], scalar1=float(128 ** -0.5), scalar2=None, op0=ALU.mult), reads=[ct], writes=[ct])
        P.op("act", lambda e: e.activation(out=c32[:, C_ESINK:C_ESINK + 8], in_=c32[:, SP_SK:SP_SK + 8], func=AF.Exp), reads=[ct], writes=[ct])
        if "dbg" in self.debug:
            self.C32o = self.ext_out("C32o", [128, 1200])
            P.dma("act", self.C32o[:, :], c32[:, :], reads=[ct], writes=[P.tk()])
        P.barrier()

    def qknorm(self, ps, pst, T, gcol, bufs, bank=5):
        P = self.P
        sqq, t_sqq, rs, t_rs, qn, t_qn = bufs
        ps5, ps5t = self.psum[bank], self.psum_t[bank]
        P.op("act", lambda e: e.activation(out=sqq[:, :T], in_=ps[:, :T], func=AF.Square), reads=[pst], writes=[t_sqq])
        P.mm(ps5[:, :T], self.ones32[:], sqq[:, :T], True, True, reads=[t_sqq, self.consts_t], writes=[ps5t])
        P.op("act", lambda e: e.activation(out=rs[:, :T], in_=ps5[:, :T], func=AF.Sqrt, bias=self.epsc[:, 0:1], scale=1.0 / 128), reads=[ps5t, self.consts_t], writes=[t_rs])
        P.op("dve", lambda e: e.reciprocal(out=rs[:, :T], in_=rs[:, :T]), reads=[t_rs], writes=[t_rs])
        P.op("dve", lambda e: e.scalar_tensor_tensor(out=qn[:, :T], in0=ps[:, :T], scalar=gcol, in1=rs[:, :T], op0=ALU.mult, op1=ALU.mult),
             reads=[pst, t_rs, self.consts_t], writes=[t_qn])


    def proj_pipeline(self, wname, tiles, h3, t_h, T, cs=None, t_cs=None):
        P = self.P
        A32, A16 = self.A32, self.A16
        c16, ct = self.cst16, self.consts_t
        o32 = [A32.take(512) for _ in range(2)]; t_o32 = [P.tk() for _ in range(2)]
        sqq = [A32.take(512) for _ in range(2)]; t_sqq = [P.tk() for _ in range(2)]
        rs = A32.take(512); t_rs = P.tk()
        qn = [A32.take(512) for _ in range(3)]; t_qn = [P.tk() for _ in range(3)]
        tt1 = A32.take(512); t_tt1 = P.tk()
        tt2 = A32.take(512); t_tt2 = P.tk()
        o16 = [A16.take(512) for _ in range(2)]; t_o16 = [P.tk() for _ in range(2)]
        q16 = [A16.take(512) for _ in range(2)]; t_q16 = [P.tk() for _ in range(2)]
        nbank = [5, 0]

        def store(tl, o, to):
            P.dma("pool", tl["dst"], o[:, :T], reads=[to], writes=[tl["td"]], partial=True, pn="pst")

        def s0(i, tl):
            wt, tw = self.wtile(wname, tl["n"])
            ps, pst = self.psum[1 + i % 4], self.psum_t[1 + i % 4]
            for kc in range(KC):
                P.mm(ps[:, :T], wt[:, kc, :], h3[:, kc, :], kc == 0, kc == KC - 1, reads=[tw, t_h], writes=[pst])
            if tl["kind"] in ("copy", "gelu"):
                o, to = o32[i % 2], t_o32[i % 2]
                fn = AF.Copy if tl["kind"] == "copy" else AF.Gelu_apprx_tanh
                P.op("act", lambda e, o=o, ps=ps, fn=fn: e.activation(out=o[:, :T], in_=ps[:, :T], func=fn), reads=[pst], writes=[to])
                store(tl, o, to)
            else:
                P.op("act", lambda e, o=sqq[i % 2], ps=ps: e.activation(out=o[:, :T], in_=ps[:, :T], func=AF.Square), reads=[pst], writes=[t_sqq[i % 2]])

        def s1(i, tl):
            if tl["kind"] != "qk":
                return
            ps, pst = self.psum[1 + i % 4], self.psum_t[1 + i % 4]
            pn_, pnt = self.psum[nbank[i % 2]], self.psum_t[nbank[i % 2]]
            P.mm(pn_[:, :T], self.ones32[:], sqq[i % 2][:, :T], True, True, reads=[t_sqq[i % 2], ct], writes=[pnt])
            P.op("act", lambda e, pn_=pn_: e.activation(out=rs[:, :T], in_=pn_[:, :T], func=AF.Sqrt, bias=self.epsc[:, 0:1], scale=1.0 / 128), reads=[pnt, ct], writes=[t_rs])
            P.op("dve", lambda e: e.reciprocal(out=rs[:, :T], in_=rs[:, :T]), reads=[t_rs], writes=[t_rs])
            Q, tQ = qn[i % 3], t_qn[i % 3]
            P.op("dve", lambda e, Q=Q, ps=ps, g=tl["gcol"]: e.scalar_tensor_tensor(out=Q[:, :T], in0=ps[:, :T], scalar=g, in1=rs[:, :T], op0=ALU.mult, op1=ALU.mult),
                 reads=[pst, t_rs, ct], writes=[tQ])
            if tl["rope"]:
                P.op("act", lambda e, Q=Q, o=q16[i % 2]: e.activation(out=o[:, :T], in_=Q[:, :T], func=AF.Copy), reads=[tQ], writes=[t_q16[i % 2]])
            else:
                o, to = o16[i % 2], t_o16[i % 2]
                P.op("act", lambda e, Q=Q, o=o: e.activation(out=o[:, :T], in_=Q[:, :T], func=AF.Copy), reads=[tQ], writes=[to])
                store(tl, o, to)

        def s2(i, tl):
            if tl["kind"] != "qk" or not tl["rope"]:
                return
            Q, tQ = qn[i % 3], t_qn[i % 3]
            ps6, ps6t = self.psum[6], self.psum_t[6]
            P.mm(ps6[:, :T], c16[:, 0:128], q16[i % 2][:, :T], True, True, reads=[t_q16[i % 2], ct], writes=[ps6t])
            P.op("dve", lambda e, Q=Q: e.tensor_tensor(out=tt1[:, :T], in0=Q[:, :T], in1=cs[:, 0:T], op=ALU.mult), reads=[tQ, t_cs], writes=[t_tt1])
            P.op("dve", lambda e: e.tensor_tensor(out=tt2[:, :T], in0=ps6[:, :T], in1=cs[:, 512:512 + T], op=ALU.mult), reads=[ps6t, t_cs], writes=[t_tt2])
            o, to = o16[i % 2], t_o16[i % 2]
            P.op("dve", lambda e, o=o: e.tensor_tensor(out=o[:, :T], in0=tt1[:, :T], in1=tt2[:, :T], op=ALU.add), reads=[t_tt1, t_tt2], writes=[to])
            store(tl, o, to)

        nt = len(tiles)
        for step in range(nt + 2):
            if step < nt:
                s0(step, tiles[step])
            if 0 <= step - 1 < nt:
                s1(step - 1, tiles[step - 1])
            if 0 <= step - 2 < nt:
                s2(step - 2, tiles[step - 2])

    def mixer_ab(self):
        P = self.P
        A32, A16 = self.A32, self.A16
        c32, c16, ct = self.cst32, self.cst16, self.consts_t
        l = 0
        ch = self.chunks()
        t_XL = P.tk(phase=False); t_GG = P.tk(phase=False); t_QT = P.tk(phase=False); t_KT = P.tk(phase=False)
        t_VV = P.tk(phase=False); t_HP = P.tk(phase=False); t_CAT = P.tk(phase=False)
        for ci, (t0, T, w) in enumerate(ch):
            A32.reset(); A16.reset()
            x32 = A32.take(KC * 512); x3 = x32[:, :KC * T].rearrange("p (k t) -> p k t", t=T); t_x = P.tk()
            h16 = A16.take(KC * 512); h3 = h16[:, :KC * T].rearrange("p (k t) -> p k t", t=T); t_h = P.tk()
            P.dma("sp", x3, self.XT[:, t0:t0 + T].rearrange("(k p) t -> p k t", p=128), reads=[self.xt_tk(ci)], writes=[t_x])
            self.norm_mod(x3, t_x, h3, t_h, T, l, w, 1)
            cs = A32.take(1024); t_cs = P.tk()
            if not w:
                P.dma("sp", cs[:, 0:T], self.ropeT[:, t0:t0 + T], writes=[t_cs])
                P.dma("sp", cs[:, 512:512 + T], self.ropeT[:, LAT + t0:LAT + t0 + T], writes=[t_cs], partial=True)
            v16 = [A16.take(256) for _ in range(2)]; t_v16 = [P.tk() for _ in range(2)]
            ext = w or (t0 // 512) < NCH_EXT
            tiles = []
            for n in range(26 if ext else 8):
                if n < 8:
                    tiles.append(dict(n=n, kind="copy", dst=self.XL[n * 128:(n + 1) * 128, t0:t0 + T], td=t_XL))
                elif n < 16:
                    tiles.append(dict(n=n, kind="gelu", dst=self.GG[(n - 8) * 128:(n - 7) * 128, t0:t0 + T], td=t_GG))
                elif n < 24:
                    tiles.append(dict(n=n, kind="qk", gcol=c32[:, C_GQS:C_GQS + 1], rope=not w, dst=self.QT[(n - 16) * 128:(n - 15) * 128, t0:t0 + T], td=t_QT))
                else:
                    tiles.append(dict(n=n, kind="qk", gcol=c32[:, SP_KG:SP_KG + 1], rope=not w, dst=self.KT[(n - 24) * 128:(n - 23) * 128, t0:t0 + T], td=t_KT))
            self.proj_pipeline("ab_in", tiles, h3, t_h, T, cs, t_cs)
            if not ext:
                P.barrier()
                continue
            wv0, tv0 = self.wtile("ab_in", 26)
            wv1, tv1 = self.wtile("ab_in", 27)
            ps7, ps7t = self.psum[7], self.psum_t[7]
            for tb in range(T // 128):
                for hv, (wv, tv) in enumerate(((wv0, tv0), (wv1, tv1))):
                    for kc in range(KC):
                        P.mm(ps7[:, hv * 128:(hv + 1) * 128], h3[:, kc, tb * 128:(tb + 1) * 128], wv[:, kc, :], kc == 0, kc == KC - 1,
                             reads=[tv, t_h], writes=[ps7t])
                vo, tvo = v16[tb % 2], t_v16[tb % 2]
                P.op("act", lambda e, vo=vo: e.activation(out=vo, in_=ps7[:, 0:256], func=AF.Copy), reads=[ps7t], writes=[tvo])
                P.dma("pool", self.VV[t0 + tb * 128:t0 + (tb + 1) * 128, :], vo, reads=[tvo], writes=[t_VV], partial=True, pn="pst")
            P.barrier()
        for t in (t_XL, t_GG, t_QT, t_KT, t_VV):
            t.w = []; t.r = {}
        lat_ch = [c for c in ch if not c[2]]
        ctx_ch = [c for c in ch if c[2]]
        gw3 = c16[:, 256:256 + 4096].rearrange("p (k m n) -> p k m n", k=2, m=16)
        A32.reset(); A16.reset()
        def bufs32(cnt, n_=512):
            return [A32.take(n_) for _ in range(cnt)], [P.tk() for _ in range(cnt)]
        Xb, tXb = bufs32(2, 516)
        Ub, tUb = bufs32(5)
        Rb, tRb = bufs32(3)
        Ib, tIb = bufs32(3)
        A2b, tA2b = bufs32(2)
        Hb, tHb = bufs32(2)
        HPb, tHPb = bufs32(2)
        GGb, tGGb = bufs32(2)
        U16b = [A16.take(512) for _ in range(2)]; tU16b = [P.tk() for _ in range(2)]
        OCb = [A16.take(512) for _ in range(2)]; tOCb = [P.tk() for _ in range(2)]
        carry = A32.take(8); t_carry = [P.tk() for _ in range(8)]
        mod_gc = [0]

        def st0(j, d, idx, t0, T, w, n):
            seg0, seg1 = (LAT, LAT + CTX) if w else (0, LAT)
            lo, hi = max(seg0, t0 - 2), min(seg1, t0 + T + 2)
            rows = slice(n * 128, (n + 1) * 128)
            X, tX = Xb[j % 2], tXb[j % 2]
            if lo != t0 - 2 or hi != t0 + T + 2:
                P.op("dve", lambda e, X=X: e.memset(X[:, 0:516], 0.0), writes=[tX])
            P.dma("sp", X[:, lo - (t0 - 2):hi - (t0 - 2)], self.XL[rows, lo:hi], reads=[t_XL], writes=[tX])
            U, tU = Ub[j % 5], tUb[j % 5]
            P.op("dve", lambda e, X=X, U=U, T=T, n=n: e.tensor_scalar(out=U[:, :T], in0=X[:, 0:T], scalar1=c32[:, SP_CW + n * 5:SP_CW + n * 5 + 1],
                                                         scalar2=c32[:, SP_CB + n:SP_CB + n + 1], op0=ALU.mult, op1=ALU.add), reads=[tX, ct], writes=[tU])
            for kk in range(1, 5):
                P.op("dve", lambda e, X=X, U=U, kk=kk, T=T, n=n: e.scalar_tensor_tensor(out=U[:, :T], in0=X[:, kk:kk + T], scalar=c32[:, SP_CW + n * 5 + kk:SP_CW + n * 5 + kk + 1],
                                                                           in1=U[:, :T], op0=ALU.mult, op1=ALU.add), reads=[tX, ct, tU], writes=[tU])

        def st1(j, d, idx, t0, T, w, n):
            U, tU, U16, tU16 = Ub[j % 5], tUb[j % 5], U16b[j % 2], tU16b[j % 2]
            P.op("act", lambda e, U=U, U16=U16, T=T: e.activation(out=U16[:, :T], in_=U[:, :T], func=AF.Copy), reads=[tU], writes=[tU16])
            k = j % 2
            psA, psAt = self.psum[1 + 2 * k], self.psum_t[1 + 2 * k]
            psX, psXt = self.psum[2 + 2 * k], self.psum_t[2 + 2 * k]
            P.mm(psA[:, :T], gw3[:, 0, d * 8 + n, :], U16[:, :T], True, True, reads=[tU16, ct], writes=[psAt])
            P.mm(psX[:, :T], gw3[:, 1, d * 8 + n, :], U16[:, :T], True, True, reads=[tU16, ct], writes=[psXt])

        def st2(j, d, idx, t0, T, w, n):
            dn = d * 8 + n
            k = j % 2
            psA, psAt = self.psum[1 + 2 * k], self.psum_t[1 + 2 * k]
            psX, psXt = self.psum[2 + 2 * k], self.psum_t[2 + 2 * k]
            R, tR, I, tI = Rb[j % 3], tRb[j % 3], Ib[j % 3], tIb[j % 3]
            P.op("act", lambda e, psA=psA, R=R, T=T, dn=dn: e.activation(out=R[:, :T], in_=psA[:, :T], func=AF.Sigmoid, bias=c32[:, SP_BA + dn:SP_BA + dn + 1], scale=1.0), reads=[psAt, ct], writes=[tR])
            P.op("act", lambda e, psX=psX, I=I, T=T, dn=dn: e.activation(out=I[:, :T], in_=psX[:, :T], func=AF.Sigmoid, bias=c32[:, SP_BX + dn:SP_BX + dn + 1], scale=1.0), reads=[psXt, ct], writes=[tI])
            P.op("act", lambda e, R=R, T=T, dn=dn: e.activation(out=R[:, :T], in_=R[:, :T], func=AF.Exp, scale=c32[:, C_SP8 + dn:C_SP8 + dn + 1]), reads=[tR, ct], writes=[tR])

        def st3(j, d, idx, t0, T, w, n):
            R, tR, A2, tA2 = Rb[j % 3], tRb[j % 3], A2b[j % 2], tA2b[j % 2]
            P.op("dve", lambda e, R=R, A2=A2, T=T: e.tensor_tensor(out=A2[:, :T], in0=R[:, :T], in1=R[:, :T], op=ALU.mult), reads=[tR], writes=[tA2])
            P.op("act", lambda e, A2=A2, T=T: e.activation(out=A2[:, :T], in_=A2[:, :T], func=AF.Sqrt, bias=self.ones32[:, 0:1], scale=-1.0), reads=[tA2, ct], writes=[tA2])

        def st4(j, d, idx, t0, T, w, n):
            rows = slice(n * 128, (n + 1) * 128)
            need_out = w or (t0 // 512) < NCH_EXT
            R, tR, I, tI, A2, tA2 = Rb[j % 3], tRb[j % 3], Ib[j % 3], tIb[j % 3], A2b[j % 2], tA2b[j % 2]
            U, tU, H, tH = Ub[j % 5], tUb[j % 5], Hb[j % 2], tHb[j % 2]
            P.op("dve", lambda e, I=I, A2=A2, T=T: e.tensor_tensor(out=I[:, :T], in0=A2[:, :T], in1=I[:, :T], op=ALU.mult), reads=[tA2, tI], writes=[tI])
            P.op("dve", lambda e, I=I, U=U, T=T: e.tensor_tensor(out=I[:, :T], in0=I[:, :T], in1=U[:, :T], op=ALU.mult), reads=[tI, tU], writes=[tI])
            if idx == 0:
                init, rd = 0.0, []
            else:
                init, rd = carry[:, n:n + 1], [t_carry[n]]
            if d == 0:
                P.op("dve", lambda e, H=H, R=R, I=I, init=init, T=T: e.tensor_tensor_scan(out=H[:, :T], data0=R[:, :T], data1=I[:, :T], initial=init, op0=ALU.mult, op1=ALU.add),
                     reads=[tR, tI] + rd, writes=[tH])
                P.op("dve", lambda e, H=H, T=T, n=n: e.tensor_copy(out=carry[:, n:n + 1], in_=H[:, T - 1:T]), reads=[tH], writes=[t_carry[n]])
                if need_out:
                    P.dma("pool", self.HP[rows, t0:t0 + T], H[:, :T], reads=[tH], writes=[t_HP], partial=True, pn="pst")
            else:
                P.op("dve", lambda e, H=H, R=R, I=I, init=init, T=T: e.tensor_tensor_scan(out=H[:, T - 1::-1], data0=R[:, T - 1::-1], data1=I[:, T - 1::-1], initial=init, op0=ALU.mult, op1=ALU.add),
                     reads=[tR, tI] + rd, writes=[tH])
                P.op("dve", lambda e, H=H, n=n: e.tensor_copy(out=carry[:, n:n + 1], in_=H[:, 0:1]), reads=[tH], writes=[t_carry[n]])
                if not need_out:
                    return
                HPc, tHP, GGc, tGG = HPb[j % 2], tHPb[j % 2], GGb[j % 2], tGGb[j % 2]
                P.dma("sp", HPc[:, :T], self.HP[rows, t0:t0 + T], reads=[t_HP], writes=[tHP])
                P.dma("sp", GGc[:, :T], self.GG[rows, t0:t0 + T], reads=[t_GG], writes=[tGG])
                P.op("dve", lambda e, H=H, HPc=HPc, T=T: e.tensor_tensor(out=HPc[:, :T], in0=H[:, :T], in1=HPc[:, :T], op=ALU.add), reads=[tH, tHP], writes=[tHP])
                OC, tOC = OCb[j % 2], tOCb[j % 2]
                P.op("dve", lambda e, OC=OC, GGc=GGc, HPc=HPc, T=T: e.tensor_tensor(out=OC[:, :T], in0=HPc[:, :T], in1=GGc[:, :T], op=ALU.mult), reads=[tHP, tGG], writes=[tOC])
                P.dma("pool", self.CAT[rows, t0:t0 + T], OC[:, :T], reads=[tOC], writes=[t_CAT], partial=True, pn="pst")

        stages = [st0, st1, st2, st3, st4]
        for d in range(2):
            order = ctx_ch + (lat_ch if d == 0 else lat_ch[::-1])
            jobs = [(d, idx, t0, T, w, n) for idx, (t0, T, w) in enumerate(order) for n in range(8)]
            for step in range(len(jobs) + len(stages) - 1):
                for si, st in enumerate(stages):
                    j = step - si
                    if 0 <= j < len(jobs):
                        st(j, *jobs[j])
                if mod_gc[0] < 144:
                    self.mod_tiles(1, [mod_gc[0]], bank=0)
                    mod_gc[0] += 1
            self._m0b_barrier(t_HP)
        assert mod_gc[0] == 144, mod_gc[0]
        self.mod_finish(1, bank=0)
        self._m0b_barrier(t_HP)
        NB = NT // 128
        NLB = LAT // 128
        for g in range(2):
            A32.reset(); A16.reset()
            K16 = A16.take(NT); t_K = P.tk()
            V16 = A16.take(NT); t_V = P.tk()
            Q16 = A16.take(4 * NT); t_Q = P.tk()
            V3 = V16.rearrange("p (b c) -> p b c", c=128)
            Q3 = Q16.rearrange("p (j t) -> p j t", j=4)
            P.dma("sp", K16, self.KT[g * 128:(g + 1) * 128, :], reads=[t_KT], writes=[t_K])
            P.dma("sp", V3, self.VV.rearrange("(b p) c -> p b c", p=128)[:, :, g * 128:(g + 1) * 128], reads=[t_VV], writes=[t_V])
            P.dma("sp", Q3, self.QT[4 * g * 128:(4 * g + 4) * 128, :].rearrange("(j p) t -> p j t", p=128), reads=[t_QT], writes=[t_Q])
            esr = A32.take(512); t_esr = P.tk()
            for j in range(4):
                P.op("dve", lambda e, j=j, g=g: e.tensor_scalar(out=esr[:, j * 128:(j + 1) * 128], in0=self.ones32[:, 0:128], scalar1=c32[:, C_ESINK + 4 * g + j:C_ESINK + 4 * g + j + 1],
                                                        scalar2=None, op0=ALU.mult), reads=[ct, t_esr], writes=[t_esr])
            Es = [[A16.take(512) for _ in range(5)] for _ in range(2)]; t_Es = [[P.tk() for _ in range(5)] for _ in range(2)]
            sm = [A32.take(512) for _ in range(2)]; t_sm = [P.tk() for _ in range(2)]
            den = A32.take(512); t_den = P.tk()
            ob = [A16.take(512) for _ in range(2)]; t_ob = [P.tk() for _ in range(2)]
            qbl = list(range(NCH_EXT * 4 if HALF else NLB))
            if HALF:
                qbl = qbl[:18]
            qjobs = qbl + [NLB, NLB + 1]
            nmc = [0]

            def kbs_of(qb):
                kbs = []
                if qb < NLB:
                    if qb > 0:
                        kbs.append((qb - 1, "P"))
                    kbs.append((qb, "C"))
                    if qb < NLB - 1:
                        kbs.append((qb + 1, "N"))
                return kbs + [(NLB, "C"), (NLB + 1, "C")]

            def stA(t, qb):
                E, t_E = Es[t % 2], t_Es[t % 2]
                Q4 = Q3[:, :, qb * 128:(qb + 1) * 128]
                for si, (kb, kind) in enumerate(kbs_of(qb)):
                    pS, pSt = self.psum[si], self.psum_t[si]
                    P.mm(pS[:, :].rearrange("p (j t) -> p j t", j=4), K16[:, kb * 128:(kb + 1) * 128], Q4, True, True, reads=[t_K, t_Q], writes=[pSt])
                    if kind == "C":
                        P.op("act", lambda e, pS=pS, Ei=E[si]: e.activation(out=Ei, in_=pS[:, :], func=AF.Exp), reads=[pSt], writes=[t_E[si]])
                    else:
                        mk = c32[:, C_MASK:C_MASK + 512] if kind == "P" else c32[:, C_MASK + 512:C_MASK + 1024]
                        S_, tS = sm[nmc[0] % 2], t_sm[nmc[0] % 2]; nmc[0] += 1
                        P.op("dve", lambda e, pS=pS, S_=S_, mk=mk: e.tensor_tensor(out=S_, in0=pS[:, :], in1=mk, op=ALU.add), reads=[pSt, ct], writes=[tS])
                        P.op("act", lambda e, S_=S_, Ei=E[si]: e.activation(out=Ei, in_=S_, func=AF.Exp), reads=[tS], writes=[t_E[si]])

            def stB(t, qb, g=g):
                E, t_E = Es[t % 2], t_Es[t % 2]
                kbs = kbs_of(qb)
                pO, pOt = self.psum[5], self.psum_t[5]
                pD, pDt = self.psum[6], self.psum_t[6]
                for si, (kb, kind) in enumerate(kbs):
                    P.mm(pO[:, :], V3[:, kb, :], E[si], si == 0, si == len(kbs) - 1, reads=[t_V, t_E[si]], writes=[pOt])
                for si, (kb, kind) in enumerate(kbs):
                    P.mm(pD[:, :], self.ones16[:], E[si], si == 0, si == len(kbs) - 1, reads=[ct, t_E[si]], writes=[pDt])
                P.op("dve", lambda e, pD=pD: e.tensor_tensor(out=den, in0=pD[:, :], in1=esr, op=ALU.add), reads=[pDt, t_esr], writes=[t_den])
                P.op("dve", lambda e: e.reciprocal(out=den, in_=den), reads=[t_den], writes=[t_den])
                O, tO = ob[t % 2], t_ob[t % 2]
                P.op("dve", lambda e, O=O, pO=pO: e.tensor_tensor(out=O, in0=pO[:, :], in1=den, op=ALU.mult), reads=[pOt, t_den], writes=[tO])
                r0 = (8 + 4 * g) * 128
                P.dma("pool", self.CAT[r0:r0 + 512, qb * 128:(qb + 1) * 128].rearrange("(j p) t -> p j t", p=128), O.rearrange("p (j t) -> p j t", j=4),
                      reads=[tO], writes=[t_CAT], partial=True, pn="pst")

            for t in range(len(qjobs) + 1):
                if t < len(qjobs):
                    stA(t, qjobs[t])
                if t >= 1:
                    stB(t - 1, qjobs[t - 1])
            P.barrier()
        t_CAT.w = []; t_CAT.r = {}
        self.out_proj("ab_out", l, with_ctx=True, nlat=NCH_EXT)

    def _m0b_barrier(self, t_HP):
        P = self.P
        keep = [t for t in P.phase_tiles]
        P.barrier()
        for t in keep:
            P.phase_tiles.append(t)
        t_HP.w = []; t_HP.r = {}

    def out_proj(self, wname, l, with_ctx, nlat=NCH_ALL):
        P = self.P
        A32, A16 = self.A32, self.A16
        ch = self.chunks(with_ctx, nlat)
        for ci, (t0, T, w) in enumerate(ch):
            A32.reset(); A16.reset()
            x32 = A32.take(KC * 512); x3 = x32[:, :KC * T].rearrange("p (k t) -> p k t", t=T); t_x = P.tk()
            c16 = A16.take(KC * 512); c3 = c16[:, :KC * T].rearrange("p (k t) -> p k t", t=T); t_c = P.tk()
            yo = [A32.take(512), A32.take(512)]; t_yo = [P.tk(), P.tk()]
            P.dma("sp", x3, self.XT[:, t0:t0 + T].rearrange("(k p) t -> p k t", p=128), reads=[self.xt_tk(ci)], writes=[t_x])
            P.dma("sp", c3, self.CAT[:, t0:t0 + T].rearrange("(k p) t -> p k t", p=128), writes=[t_c])
            Gm = self.mod(l, w, 1, "G")
            dst3 = self.XT[:, t0:t0 + T].rearrange("(k p) t -> p k t", p=128)
            for i in range(KC):
                wo, to = self.wtile(wname, i)
                py, pyt = self.psum[5 + i % 2], self.psum_t[5 + i % 2]
                for j in range(KC):
                    P.mm(py[:, :T], wo[:, j, :], c3[:, j, :], j == 0, j == KC - 1, reads=[to, t_c], writes=[pyt])
                P.op("dve", lambda e, o=yo[i % 2][:, :T], y=py[:, :T], g=Gm[:, i:i + 1], x=x3[:, i, :]: e.scalar_tensor_tensor(
                    out=o, in0=y, scalar=g, in1=x, op0=ALU.mult, op1=ALU.add), reads=[pyt, t_x, self.consts_t], writes=[t_yo[i % 2]])
                P.dma("act", dst3[:, i, :], yo[i % 2][:, :T], reads=[t_yo[i % 2]], writes=[self.xt_tk(ci)], partial=True)
            P.barrier()


    def mixer_na(self):
        P = self.P
        A32, A16 = self.A32, self.A16
        c32, c16, ct = self.cst32, self.cst16, self.consts_t
        l = 1
        ch = self.chunks(True, NCH_EXT)
        t_Q = P.tk(phase=False); t_K = P.tk(phase=False); t_V = P.tk(phase=False); t_CAT = P.tk(phase=False)
        QT1, KT1, VV1 = self.QT1, self.KT1, self.VV1
        for ci, (t0, T, w) in enumerate(ch):
            A32.reset(); A16.reset()
            x32 = A32.take(KC * 512); x3 = x32[:, :KC * T].rearrange("p (k t) -> p k t", t=T); t_x = P.tk()
            h16 = A16.take(KC * 512); h3 = h16[:, :KC * T].rearrange("p (k t) -> p k t", t=T); t_h = P.tk()
            P.dma("sp", x3, self.XT[:, t0:t0 + T].rearrange("(k p) t -> p k t", p=128), reads=[self.xt_tk(ci)], writes=[t_x])
            self.norm_mod(x3, t_x, h3, t_h, T, l, w, 1)
            v16 = [A16.take(512) for _ in range(2)]; t_v16 = [P.tk() for _ in range(2)]
            own = (not w) and (t0 // 512) < NCH_OWN
            tiles = []
            for n in range(0 if own else 16, 32):
                isq = n < 16
                gcol = c32[:, C_NQS:C_NQS + 1] if isq else c32[:, SP_NKG:SP_NKG + 1]
                dst, td = (QT1, t_Q) if isq else (KT1, t_K)
                r0 = (n % 16) * 128
                tiles.append(dict(n=n, kind="qk", gcol=gcol, rope=False, dst=dst[r0:r0 + 128, t0:t0 + T], td=td))
            self.proj_pipeline("na_in", tiles, h3, t_h, T)
            ps7, ps7t = self.psum[7], self.psum_t[7]
            ps6, ps6t = self.psum[6], self.psum_t[6]
            NTB = T // 128
            for n in range(16):
                wv, tv = self.wtile("na_in", 32 + n)
                pv, pvt = (ps7, ps7t) if n % 2 else (ps6, ps6t)
                for tb in range(NTB):
                    for kc in range(KC):
                        P.mm(pv[:, tb * 128:(tb + 1) * 128], h3[:, kc, tb * 128:(tb + 1) * 128], wv[:, kc, :], kc == 0, kc == KC - 1,
                             reads=[tv, t_h], writes=[pvt])
                vo, tvo = v16[n % 2], t_v16[n % 2]
                P.op("act", lambda e, vo=vo, pv=pv, T=T: e.activation(out=vo[:, :T], in_=pv[:, :T], func=AF.Copy), reads=[pvt], writes=[tvo])
                P.dma("pool", VV1[t0:t0 + T, n * 128:(n + 1) * 128].rearrange("(tb p) c -> p tb c", p=128),
                      vo[:, :T].rearrange("p (tb c) -> p tb c", c=128), reads=[tvo], writes=[t_V], partial=True, pn="pst")
            P.barrier()
        for t in (t_Q, t_K, t_V):
            t.w = []; t.r = {}
        NLB = LAT // 128
        NQB = OWN // 128
        A32.reset(); A16.reset()
        Kb = [A16.take(NT) for _ in range(2)]; tKb = [P.tk() for _ in range(2)]
        Qb_ = [A16.take(OWN) for _ in range(2)]; tQb = [P.tk() for _ in range(2)]
        Vb = [A16.take(NT) for _ in range(2)]; tVb = [P.tk() for _ in range(2)]
        OHb = [A16.take(OWN) for _ in range(2)]; tOHb = [[P.tk() for _ in range(NQB)] for _ in range(2)]
        TTb = [A32.take(15 * 64) for _ in range(2)]; tTTb = [P.tk() for _ in range(2)]
        TMb = [A16.take(9 * 128) for _ in range(2)]; tTMb = [[P.tk() for _ in range(9)] for _ in range(2)]
        MK = A32.take(3 * 128); t_MK = P.tk()
        E = [A16.take(8 * 128) for _ in range(2)]; t_E = [[P.tk(), P.tk()] for _ in range(2)]
        den = [A32.take(128) for _ in range(2)]; t_den = [P.tk() for _ in range(2)]
        ident = c16[:, 128:256]
        P.dma("sp", MK, self.namask[:, :], writes=[t_MK])

        def prologue(h):
            hp_ = h % 2
            P.dma("sp", Kb[hp_], KT1[h * 128:(h + 1) * 128, :], reads=[t_K], writes=[tKb[hp_]])
            P.dma("sp", Qb_[hp_], QT1[h * 128:(h + 1) * 128, 0:OWN], reads=[t_Q], writes=[tQb[hp_]])
            P.dma("sp", Vb[hp_].rearrange("p (b c) -> p b c", c=128), VV1.rearrange("(b p) c -> p b c", p=128)[:, :, h * 128:(h + 1) * 128], reads=[t_V], writes=[tVb[hp_]])
            P.dma("sp", TTb[hp_], self.natt[:, h * 960:(h + 1) * 960], writes=[tTTb[hp_]])
            for slot in range(9):
                o = slot - 3 if slot < 7 else (-2 if slot == 7 else 2)
                mi = 0 if slot < 7 else (1 if slot == 7 else 2)
                i0 = 7 - 2 * o
                P.op("dve", lambda e, slot=slot, i0=i0, mi=mi, TM=TMb[hp_], TT=TTb[hp_]: e.tensor_tensor(
                    out=TM[:, slot * 128:(slot + 1) * 128], in0=TT[:, i0 * 64:(i0 + 2) * 64], in1=MK[:, mi * 128:(mi + 1) * 128], op=ALU.add),
                    reads=[tTTb[hp_], t_MK], writes=[tTMb[hp_][slot]])

        def kbs_of(qb):
            if qb <= 1:
                return [0, 1, 2, 3]
            if qb >= NLB - 2:
                return [NLB - 4, NLB - 3, NLB - 2, NLB - 1]
            return [qb - 2, qb - 1, qb, qb + 1, qb + 2]

        def banks(t):
            pb = 4 * (t % 2)
            return [(self.psum[pb + i], self.psum_t[pb + i]) for i in range(4)]

        def stA(t, h, qb):
            hp_ = h % 2
            K16, Q16, TM, tms = Kb[hp_], Qb_[hp_], TMb[hp_], tTMb[hp_]
            (bA, bAt), (bB, bBt), _, _ = banks(t)
            edge = qb in (0, 1, NLB - 2, NLB - 1)
            kbs = kbs_of(qb)
            nb = len(kbs)
            Qb = Q16[:, qb * 128:(qb + 1) * 128]
            E_, tE = E[t % 2], t_E[t % 2]
            for si, kb in enumerate(kbs):
                bank, bt, col = (bA, bAt, si * 128) if si < 4 else (bB, bBt, 0)
                o = kb - qb
                slot = o + 3
                if not edge and o == -2:
                    slot = 7
                if not edge and o == 2:
                    slot = 8
                P.mm(bank[:, col:col + 128], K16[:, kb * 128:(kb + 1) * 128], Qb, True, False, reads=[tKb[hp_], tQb[hp_]], writes=[bt], ms=False)
                P.mm(bank[:, col:col + 128], ident, TM[:, slot * 128:(slot + 1) * 128], False, True, reads=[ct, tms[slot]], writes=[bt])
            for ci_, kb in enumerate((NLB, NLB + 1)):
                P.mm(bB[:, 256 + ci_ * 128:256 + (ci_ + 1) * 128], K16[:, kb * 128:(kb + 1) * 128], Qb, True, True, reads=[tKb[hp_], tQb[hp_]], writes=[bBt])
            na = min(nb, 4) * 128
            P.op("act", lambda e, bA=bA, E_=E_, na=na: e.activation(out=E_[:, 0:na], in_=bA[:, 0:na], func=AF.Exp), reads=[bAt], writes=[tE[0]])
            lo = 0 if nb == 5 else 256
            P.op("act", lambda e, bB=bB, E_=E_, lo=lo: e.activation(out=E_[:, 512 + lo:1024], in_=bB[:, lo:512], func=AF.Exp), reads=[bBt], writes=[tE[1]])

        def stB(t, h, qb):
            hp_ = h % 2
            V3 = Vb[hp_].rearrange("p (b c) -> p b c", c=128)
            _, _, (bO, bOt), (bD, bDt) = banks(t)
            kbs = kbs_of(qb)
            nb = len(kbs)
            E_, tE = E[t % 2], t_E[t % 2]
            slots = list(range(min(nb, 4))) + ([4] if nb == 5 else []) + [6, 7]
            allk = kbs + [NLB, NLB + 1]
            for si, (kb, sl) in enumerate(zip(allk, slots)):
                P.mm(bO[:, 0:128], V3[:, kb, :], E_[:, sl * 128:(sl + 1) * 128], si == 0, si == len(allk) - 1, reads=[tVb[hp_], tE[0], tE[1]], writes=[bOt])
            for si, (kb, sl) in enumerate(zip(allk, slots)):
                P.mm(bD[:, 0:128], self.ones16[:], E_[:, sl * 128:(sl + 1) * 128], si == 0, si == len(allk) - 1, reads=[ct, tE[0], tE[1]], writes=[bDt])
            dn_, tdn = den[t % 2], t_den[t % 2]
            OH = OHb[hp_]
            P.op("dve", lambda e, dn_=dn_, bD=bD: e.reciprocal(out=dn_, in_=bD[:, 0:128]), reads=[bDt], writes=[tdn])
            P.op("dve", lambda e, dn_=dn_, bO=bO, qb=qb, OH=OH: e.tensor_tensor(out=OH[:, qb * 128:(qb + 1) * 128], in0=bO[:, 0:128], in1=dn_, op=ALU.mult),
                 reads=[bOt, tdn], writes=[tOHb[hp_][qb]])
            if qb == NQB - 1:
                P.dma("pool", self.CAT[h * 128:(h + 1) * 128, 0:OWN], OH, reads=tOHb[hp_], writes=[t_CAT], partial=True, pn="pst")

        jobs = [(h, qb) for h in range(16) for qb in range(NQB)]
        prologue(0)
        for t in range(len(jobs) + 1):
            if t < len(jobs):
                h, qb = jobs[t]
                stA(t, h, qb)
            if t >= 1:
                stB(t - 1, *jobs[t - 1])
            if t < len(jobs) and jobs[t][1] == 0 and jobs[t][0] + 1 < 16:
                prologue(jobs[t][0] + 1)
        P.barrier()
        t_CAT.w = []; t_CAT.r = {}
        self.out_proj("na_out", l, with_ctx=False, nlat=NCH_OWN)

    def body(self):
        ch = self.chunks()

        def src0(c):
            t0, T, w = c
            return (self.ctxT[:, :] if w else self.xT[:, t0:t0 + T]), None

        def xt(c):
            t0, T, w = c
            return self.XT[:, t0:t0 + T], self.xt_tk(t0)

        if self.stop_after == "ffn00":
            def dst_dbg(c):
                t0, T, w = c
                if w:
                    return self.XT[:, t0:t0 + T], self.xt_tk(t0)
                return self.outT[:, t0:t0 + T], self.xt_tk(t0)
            self.ffn(0, 0, src0, dst_dbg)
            return
        self.ffn(0, 0, src0, xt)
        self.mixer_consts()
        self.mixer_ab()
        if self.stop_after == "mix0":
            P = self.P
            A32 = self.A32
            for ci, (t0, T, w) in enumerate(self.chunks(False, NCH_OWN)):
                A32.reset()
                x32 = A32.take(KC * 512); x3 = x32.rearrange("p (k t) -> p k t", t=512); t_x = P.tk()
                P.dma("sp", x3, self.XT[:, t0:t0 + T].rearrange("(k p) t -> p k t", p=128), writes=[t_x])
                P.dma("act", self.outT[:, t0:t0 + T].rearrange("(k p) t -> p k t", p=128), x3, reads=[t_x], writes=[self.xt_tk(ci)])
                P.barrier()
            return
        self.ffn(0, 1, xt, xt, nlat=NCH_EXT)
        self.ffn(1, 0, xt, xt, nlat=NCH_EXT)
        self.mixer_na()

        def dst_out(c):
            t0, T, w = c
            return self.outT[:, t0:t0 + T], self.xt_tk(t0)
        self.ffn(1, 1, xt, dst_out, with_ctx=False, nlat=NCH_OWN)


def _mixer_consts_host(inputs, mir):
    f = lambda a: np.ascontiguousarray(a, dtype=np.float32)
    L = SEQ
    pos = np.arange(L)
    if mir:
        pos = pos[::-1]
    prow, pcol = (pos // 64).astype(np.float32), (pos % 64).astype(np.float32)
    nf = 32
    inv = (np.float32(10000.0) ** (-np.arange(nf, dtype=np.float32) / np.float32(nf))).astype(np.float32)
    ar = (prow[None, :] * inv[:, None]).astype(np.float32)
    ac = (pcol[None, :] * inv[:, None]).astype(np.float32)
    cos = np.concatenate([np.cos(ar), np.cos(ar), np.cos(ac), np.cos(ac)], axis=0)
    sin = np.concatenate([np.sin(ar), np.sin(ar), np.sin(ac), np.sin(ac)], axis=0)
    ropeT = f(np.concatenate([cos, sin], axis=1))
    prot = np.zeros((128, 128), np.float32)
    for d in range(128):
        if (d % 64) < 32:
            prot[d + 32, d] = -1.0
        else:
            prot[d - 32, d] = 1.0
    j = np.arange(128)[:, None]
    i = np.arange(128)[None, :]
    mP = np.where(j >= i, 0.0, -BIG).astype(np.float32)
    mN = np.where(j <= i, 0.0, -BIG).astype(np.float32)
    maskPN = f(np.concatenate([np.tile(mP, (1, 4)), np.tile(mN, (1, 4))], axis=1))
    dsel = [1, 0] if mir else [0, 1]
    sp = np.zeros((128, NSP), np.float32)
    cw = inputs["lru_conv_w"][0]
    cw5 = np.zeros((5, 1024), np.float32)
    if mir:
        cw5[1:5] = cw[::-1]
    else:
        cw5[0:4] = cw
    sp[:, SP_CW:SP_CW + 40] = cw5.reshape(5, 8, 128).transpose(2, 1, 0).reshape(128, 40)
    sp[:, SP_CB:SP_CB + 8] = inputs["lru_conv_b"][0].reshape(8, 128).T
    sp[:, SP_BA:SP_BA + 16] = inputs["lru_b_a"][0][dsel].reshape(2, 8, 128).transpose(2, 0, 1).reshape(128, 16)
    sp[:, SP_BX:SP_BX + 16] = inputs["lru_b_x"][0][dsel].reshape(2, 8, 128).transpose(2, 0, 1).reshape(128, 16)
    sp[:, SP_LM:SP_LM + 16] = inputs["lru_lambda"][0][dsel].reshape(2, 8, 128).transpose(2, 0, 1).reshape(128, 16)
    sp[:, SP_QG] = inputs["attn_q_norm"][0]
    sp[:, SP_KG] = inputs["attn_k_norm"][0]
    sp[:, SP_SK:SP_SK + 8] = inputs["attn_sink"][0][None, :]
    sp[:, SP_NQG] = inputs["na_q_norm"][0]
    sp[:, SP_NKG] = inputs["na_k_norm"][0]
    wa = inputs["lru_w_a"][0][dsel]
    wx = inputs["lru_w_x"][0][dsel]
    gw = np.stack([wa.reshape(16, 128, 128), wx.reshape(16, 128, 128)], axis=0)
    gatew = f(gw.transpose(2, 0, 1, 3).reshape(128, 4096))
    rpb = inputs["na_rpb"][0]
    if mir:
        rpb = rpb[:, ::-1, ::-1]
    kc = np.arange(64)[:, None]
    qc = np.arange(64)[None, :]
    bidx = np.clip(kc - qc + 15, 0, 30)
    natt = np.zeros((128, 16, 15, 64), np.float32)
    for idx in range(15):
        natt[0:64, :, idx, :] = rpb[:, 14 - idx, :][:, bidx].transpose(1, 0, 2)
        if idx >= 1:
            natt[64:128, :, idx, :] = rpb[:, 15 - idx, :][:, bidx].transpose(1, 0, 2)
    ws = np.clip(np.arange(64) - (7 if mir else 8), 0, 48)[None, :]
    colv = (kc >= ws) & (kc < ws + 16)
    cv = np.tile(colv, (2, 2))
    khh = (np.arange(128) // 64)[:, None]
    qhh = (np.arange(128) // 64)[None, :]
    m_all = np.where(cv, 0.0, -BIG)
    if mir:
        m_m2 = np.where(cv & ((khh == 1) & (qhh == 0)), 0.0, -BIG)
        m_p2 = np.where(cv & ~((khh == 1) & (qhh == 0)), 0.0, -BIG)
    else:
        m_m2 = np.where(cv & ~((khh == 0) & (qhh == 1)), 0.0, -BIG)
        m_p2 = np.where(cv & ((khh == 0) & (qhh == 1)), 0.0, -BIG)
    namask = f(np.concatenate([m_all, m_m2, m_p2], axis=1))
    prot = f(np.concatenate([prot, np.eye(128, dtype=np.float32)], axis=1))
    return {"ropeT": ropeT, "prot": prot, "maskPN": maskPN, "smallp": sp, "gatew": gatew,
            "natt": f(natt.reshape(128, 16 * 960)), "namask": namask}


def _core_cfg(r):
    if HALF:
        return r // 2, (r % 2) == 1
    return r, (MIRROR_TEST and (r % 2) == 1)


def _host_inputs(inputs, names=None):
    f = lambda a: np.ascontiguousarray(a, dtype=np.float32)
    x, c, ctx, c_ctx = inputs["x"], inputs["c"], inputs["ctx"], inputs["c_ctx"]
    bmod = f(inputs["b_mod"].reshape(2, 144, 128).transpose(2, 0, 1).reshape(128, 2 * 144))
    normg = f(inputs["norm_g"].reshape(2, 3, KC, 128).transpose(3, 0, 1, 2).reshape(128, 2 * 3 * KC))
    wfull = {
        "f_in_00": inputs["ffn_w_in"][0, 0], "f_out_00": inputs["ffn_w_out"][0, 0],
        "f_in_01": inputs["ffn_w_in"][0, 1], "f_out_01": inputs["ffn_w_out"][0, 1],
        "f_in_10": inputs["ffn_w_in"][1, 0], "f_out_10": inputs["ffn_w_out"][1, 0],
        "f_in_11": inputs["ffn_w_in"][1, 1], "f_out_11": inputs["ffn_w_out"][1, 1],
        "ab_in": inputs["ab_w_in"][0], "ab_out": inputs["ab_w_out"][0],
        "na_in": inputs["na_w_in"][0], "na_out": inputs["na_w_out"][0],
    }
    shared = {"wmod": f(inputs["w_mod"]), "bmod": bmod, "normg": normg}
    for name, rows, cols in WSPEC:
        if names is None or ("w_" + name) in names:
            shared["w_" + name] = f(wfull[name])
    mc = {}
    maps = []
    for r in range(NCORES):
        b, mir = _core_cfg(r)
        if mir not in mc:
            mc[mir] = _mixer_consts_host(inputs, mir)
        m = dict(shared)
        m.update(mc[mir])
        xb, cb = x[b], ctx[b]
        if mir:
            xb, cb = xb[::-1], cb[::-1]
        m["xT"] = f(xb.T)
        m["ctxT"] = f(cb.T)
        crows = np.stack([c[b], c_ctx], axis=0)
        m["cT"] = f(crows.reshape(2, KC, 128).transpose(2, 1, 0).reshape(128, KC * 2))
        if names is not None:
            m = {k: v for k, v in m.items() if k in names}
        maps.append(m)
    return maps


_NC_CACHE = {}


def kernel(**inputs):
    key = "full"
    if key not in _NC_CACHE:
        _NC_CACHE[key] = Builder().build()
    nc = _NC_CACHE[key]
    maps = _host_inputs(inputs)
    res = run_bass_kernel_spmd(nc, maps, core_ids=list(range(NCORES)))
    out = np.empty((4, 4096, D), np.float32)
    for r in range(NCORES):
        b, mir = _core_cfg(r)
        o = res.results[r]["outT"].T
        if HALF:
            if mir:
                out[b, SEQ - OWN:SEQ] = o[::-1]
            else:
                out[b, 0:OWN] = o
        else:
            out[b] = o[::-1] if mir else o
    return out
```
